# Optimizing a Trainium2 kernel written in Bass

```python
import jax
import jax.numpy as jnp
from jax import lax
import numpy as np

D_MODEL = 1024
BATCH = 8
SEQ = 2048
DEPTH = 2

CTX_LEN = 256
GRID_W = 64
N_MIXERS = 2
N_REC_LAYERS = (DEPTH + 1) // 2
N_CONV_LAYERS = DEPTH // 2
D_RNN = (4 * D_MODEL // 3) // 128 * 128
N_RNN_BLOCKS = 16
RNN_BLOCK = D_RNN // N_RNN_BLOCKS
REC_CONV_W = 4
REC_CONV_PAD = (1, 2)
RG_C = 8.0
CONF_KW = 31
CONF_PAD = (CONF_KW // 2, CONF_KW // 2)
D_FF = 4 * D_MODEL
N_MOD = 6
EPS = 1e-6
POS_BASE = 10000.0

kernel_name = 'hybrid_rglru_conformer_dit_block'


def rmsnorm(x, g):
    xf = x.astype(jnp.float32)
    y = xf * lax.rsqrt(jnp.mean(xf * xf, axis=-1, keepdims=True) + EPS)
    return (y * g.astype(jnp.float32)).astype(x.dtype)


def layernorm(x, g, b):
    xf = x.astype(jnp.float32)
    mu = jnp.mean(xf, axis=-1, keepdims=True)
    var = jnp.mean(jnp.square(xf - mu), axis=-1, keepdims=True)
    y = (xf - mu) * lax.rsqrt(var + EPS)
    return (y * g.astype(jnp.float32) + b.astype(jnp.float32)).astype(x.dtype)


def modulate(h, shift, scale):
    return h * (1 + scale) + shift


def grid_pos_embed(rows, d, dtype):
    t = jnp.arange(rows * GRID_W, dtype=jnp.int32)
    row = (t // GRID_W).astype(jnp.float32)
    col = (t % GRID_W).astype(jnp.float32)
    q = d // 4
    omega = 1.0 / (POS_BASE ** (jnp.arange(q, dtype=jnp.float32) / q))
    er = row[:, None] * omega[None, :]
    ec = col[:, None] * omega[None, :]
    return jnp.concatenate([jnp.sin(er), jnp.cos(er), jnp.sin(ec), jnp.cos(ec)], axis=-1).astype(dtype)


def dwconv(x, w, b, pad):
    y = lax.conv_general_dilated(x, w[:, None, :].astype(x.dtype), window_strides=(1,), padding=[pad],
                                 dimension_numbers=('NWC', 'WIO', 'NWC'), feature_group_count=x.shape[-1])
    return y + b.astype(x.dtype)


def sq_relu_mlp(h, w_in, w_out):
    return jnp.square(jax.nn.relu(h @ w_in)) @ w_out


def block_diag(u, w, b):
    ub = u.reshape(u.shape[:-1] + (N_RNN_BLOCKS, RNN_BLOCK))
    y = jnp.einsum('bthi,hij->bthj', ub, w.astype(jnp.float32)) + b.astype(jnp.float32)
    return y.reshape(u.shape)


def rglru_coeffs(u, lam, w_a, b_a, w_x, b_x):
    uf = u.astype(jnp.float32)
    r = jax.nn.sigmoid(block_diag(uf, w_a, b_a))
    i = jax.nn.sigmoid(block_diag(uf, w_x, b_x))
    log_a = -RG_C * r * jax.nn.softplus(-lam.astype(jnp.float32))
    a = jnp.exp(log_a)
    return a, jnp.sqrt(-jnp.expm1(2.0 * log_a)) * (i * uf)


def linear_scan(a, b, h0, reverse):
    def combine(l, r):
        al, bl = l
        ar, br = r
        return al * ar, ar * bl + br
    a_cum, b_cum = lax.associative_scan(combine, (a, b), axis=1, reverse=reverse)
    return a_cum * h0[:, None, :] + b_cum


def recurrent_block(h_lat, h_ctx, w_in, conv_w, conv_b, lam, w_a, b_a, w_x, b_x, w_out, ctx_out):
    w_gate, w_rec = w_in[:, :D_RNN], w_in[:, D_RNN:]
    u_lat = dwconv(h_lat @ w_rec, conv_w, conv_b, REC_CONV_PAD)
    u_ctx = dwconv(h_ctx @ w_rec, conv_w, conv_b, REC_CONV_PAD)
    zeros = jnp.zeros((h_lat.shape[0], D_RNN), jnp.float32)
    ys_lat, ys_ctx = [], []
    for d, rev in enumerate((False, True)):
        a_c, b_c = rglru_coeffs(u_ctx, lam[d], w_a[d], b_a[d], w_x[d], b_x[d])
        s_ctx = linear_scan(a_c, b_c, zeros, rev)
        h0 = s_ctx[:, 0] if rev else s_ctx[:, -1]
        a_l, b_l = rglru_coeffs(u_lat, lam[d], w_a[d], b_a[d], w_x[d], b_x[d])
        ys_lat.append(linear_scan(a_l, b_l, h0, rev))
        ys_ctx.append(s_ctx)
    y_lat = (ys_lat[0] + ys_lat[1]).astype(h_lat.dtype)
    out_lat = (jax.nn.gelu(h_lat @ w_gate) * y_lat) @ w_out
    if not ctx_out:
        return out_lat, None
    y_ctx = (ys_ctx[0] + ys_ctx[1]).astype(h_ctx.dtype)
    out_ctx = (jax.nn.gelu(h_ctx @ w_gate) * y_ctx) @ w_out
    return out_lat, out_ctx


def conformer_conv(h, w_pw1, b_pw1, conv_w, conv_b, ln_g, ln_b, w_pw2, b_pw2):
    z = jax.nn.glu(h @ w_pw1 + b_pw1, axis=-1)
    z = dwconv(z, conv_w, conv_b, CONF_PAD)
    z = jax.nn.silu(layernorm(z, ln_g, ln_b))
    return z @ w_pw2 + b_pw2


def setup_inputs(seed: int = 0) -> dict:
    key = jax.random.key(seed)
    ks = jax.random.split(key, 32)
    f32 = jnp.float32

    def nrm(k, shape, scale):
        return jax.random.normal(k, shape, f32) * scale

    x = nrm(ks[0], (BATCH, SEQ, D_MODEL), 1.0)
    c = nrm(ks[1], (BATCH, D_MODEL), 1.0)
    ctx = nrm(ks[2], (BATCH, CTX_LEN, D_MODEL), 1.0)
    c_ctx = nrm(ks[3], (D_MODEL,), 1.0)
    w_ada = nrm(ks[4], (DEPTH, D_MODEL, N_MOD * D_MODEL), 0.5 * D_MODEL ** -0.5)
    b_ada = nrm(ks[5], (DEPTH, N_MOD * D_MODEL), 0.02)
    norm_g = 1.0 + nrm(ks[6], (DEPTH, 2, D_MODEL), 0.05)
    rec_w_in = nrm(ks[7], (N_REC_LAYERS, D_MODEL, 2 * D_RNN), D_MODEL ** -0.5)
    rec_conv_w = nrm(ks[8], (N_REC_LAYERS, REC_CONV_W, D_RNN), REC_CONV_W ** -0.5)
    rec_conv_b = nrm(ks[9], (N_REC_LAYERS, D_RNN), 0.02)
    u = jax.random.uniform(ks[10], (N_REC_LAYERS, 2, D_RNN), f32, 0.9, 0.999)
    a_base = u ** (1.0 / RG_C)
    rec_lambda = jnp.log(a_base) - jnp.log1p(-a_base)
    rec_w_a = nrm(ks[11], (N_REC_LAYERS, 2, N_RNN_BLOCKS, RNN_BLOCK, RNN_BLOCK), RNN_BLOCK ** -0.5)
    rec_b_a = nrm(ks[12], (N_REC_LAYERS, 2, N_RNN_BLOCKS, RNN_BLOCK), 0.02)
    rec_w_x = nrm(ks[13], (N_REC_LAYERS, 2, N_RNN_BLOCKS, RNN_BLOCK, RNN_BLOCK), RNN_BLOCK ** -0.5)
    rec_b_x = nrm(ks[14], (N_REC_LAYERS, 2, N_RNN_BLOCKS, RNN_BLOCK), 0.02)
    rec_w_out = nrm(ks[15], (N_REC_LAYERS, D_RNN, D_MODEL), D_RNN ** -0.5)
    conf_w_pw1 = nrm(ks[16], (N_CONV_LAYERS, D_MODEL, 2 * D_MODEL), D_MODEL ** -0.5)
    conf_b_pw1 = nrm(ks[17], (N_CONV_LAYERS, 2 * D_MODEL), 0.02)
    conf_conv_w = nrm(ks[18], (N_CONV_LAYERS, CONF_KW, D_MODEL), CONF_KW ** -0.5)
    conf_conv_b = nrm(ks[19], (N_CONV_LAYERS, D_MODEL), 0.02)
    conf_ln_g = 1.0 + nrm(ks[20], (N_CONV_LAYERS, D_MODEL), 0.05)
    conf_ln_b = nrm(ks[21], (N_CONV_LAYERS, D_MODEL), 0.02)
    conf_w_pw2 = nrm(ks[22], (N_CONV_LAYERS, D_MODEL, D_MODEL), D_MODEL ** -0.5)
    conf_b_pw2 = nrm(ks[23], (N_CONV_LAYERS, D_MODEL), 0.02)
    mlp_w_in = nrm(ks[24], (DEPTH, D_MODEL, D_FF), D_MODEL ** -0.5)
    mlp_w_out = nrm(ks[25], (DEPTH, D_FF, D_MODEL), D_FF ** -0.5)
    final_g = 1.0 + nrm(ks[26], (D_MODEL,), 0.05)
    return {'x': x, 'c': c, 'ctx': ctx, 'c_ctx': c_ctx, 'w_ada': w_ada, 'b_ada': b_ada, 'norm_g': norm_g,
            'rec_w_in': rec_w_in, 'rec_conv_w': rec_conv_w, 'rec_conv_b': rec_conv_b, 'rec_lambda': rec_lambda,
            'rec_w_a': rec_w_a, 'rec_b_a': rec_b_a, 'rec_w_x': rec_w_x, 'rec_b_x': rec_b_x, 'rec_w_out': rec_w_out,
            'conf_w_pw1': conf_w_pw1, 'conf_b_pw1': conf_b_pw1, 'conf_conv_w': conf_conv_w, 'conf_conv_b': conf_conv_b,
            'conf_ln_g': conf_ln_g, 'conf_ln_b': conf_ln_b, 'conf_w_pw2': conf_w_pw2, 'conf_b_pw2': conf_b_pw2,
            'mlp_w_in': mlp_w_in, 'mlp_w_out': mlp_w_out, 'final_g': final_g}


def reference(x, c, ctx, c_ctx, w_ada, b_ada, norm_g, rec_w_in, rec_conv_w, rec_conv_b, rec_lambda,
              rec_w_a, rec_b_a, rec_w_x, rec_b_x, rec_w_out, conf_w_pw1, conf_b_pw1, conf_conv_w, conf_conv_b,
              conf_ln_g, conf_ln_b, conf_w_pw2, conf_b_pw2, mlp_w_in, mlp_w_out, final_g):
    rows = x.shape[1] // GRID_W
    x = x + grid_pos_embed(rows, x.shape[-1], x.dtype)[None]
    xc = ctx
    last_ctx_layer = ((DEPTH - 1) // N_MIXERS) * N_MIXERS
    s_c = jax.nn.silu(c)
    s_cc = jax.nn.silu(c_ctx)
    for i in range(DEPTH):
        sh1, sc1, g1, sh2, sc2, g2 = jnp.split((s_c @ w_ada[i] + b_ada[i])[:, None, :], N_MOD, axis=-1)
        use_ctx = i <= last_ctx_layer
        ctx_out = i < last_ctx_layer
        h = modulate(rmsnorm(x, norm_g[i, 0]), sh1, sc1)
        hc = None
        if use_ctx:
            csh1, csc1, cg1, csh2, csc2, cg2 = jnp.split((s_cc @ w_ada[i] + b_ada[i])[None, None, :], N_MOD, axis=-1)
            hc = modulate(rmsnorm(xc, norm_g[i, 0]), csh1, csc1)
        j = i // N_MIXERS
        if i % N_MIXERS == 0:
            y, yc = recurrent_block(h, hc, rec_w_in[j], rec_conv_w[j], rec_conv_b[j], rec_lambda[j],
                                    rec_w_a[j], rec_b_a[j], rec_w_x[j], rec_b_x[j], rec_w_out[j], ctx_out)
        else:
            conf_p = (conf_w_pw1[j], conf_b_pw1[j], conf_conv_w[j], conf_conv_b[j],
                      conf_ln_g[j], conf_ln_b[j], conf_w_pw2[j], conf_b_pw2[j])
            y = conformer_conv(h, *conf_p)
            yc = conformer_conv(hc, *conf_p) if ctx_out else None
        x = x + g1 * y
        x = x + g2 * sq_relu_mlp(modulate(rmsnorm(x, norm_g[i, 1]), sh2, sc2), mlp_w_in[i], mlp_w_out[i])
        if ctx_out:
            xc = xc + cg1 * yc
            xc = xc + cg2 * sq_relu_mlp(modulate(rmsnorm(xc, norm_g[i, 1]), csh2, csc2), mlp_w_in[i], mlp_w_out[i])
    return rmsnorm(x, final_g)
```

```python
import math
from contextlib import ExitStack

import numpy as np
import concourse.bass as bass
import concourse.mybir as mybir
from concourse.ap import AP
from concourse.bass_utils import run_bass_kernel_spmd

F32 = mybir.dt.float32
BF16 = mybir.dt.bfloat16
I32 = mybir.dt.int32
AF = mybir.ActivationFunctionType
ALU = mybir.AluOpType

D = 1024
T_LAT = 2048
T_CTX = 256
R = 1280
NCH = 8
NR = 10
DFF = 4096
KW = 31
EPS = 1e-6
RG_C = 8.0
NSLOT = 4
NPE = 23
EPOCH = 3000


def kset(c):
    b0 = (128 * c) // 80
    b1 = (128 * c + 127) // 80
    k0 = (80 * b0) // 128
    k1 = (80 * (b1 + 1) - 1) // 128
    return list(range(k0, k1 + 1))


_PAR = {}
_off = 0
for _name, _n in [("cvec", 16), ("bada", 96), ("ng", 32), ("fg", 8), ("rcw", 40), ("rcb", 10),
                  ("lam", 20), ("ba", 20), ("bx", 20), ("bpw1", 16), ("ccw", 8 * KW), ("ccb", 8),
                  ("lng", 8), ("lnb", 8), ("bpw2", 8)]:
    _PAR[_name] = (_off, _n)
    _off += _n
NPAR = _off


class Prog:
    def __init__(self, nc, es):
        self.nc = nc
        self.es = es
        self.E = {"pe": nc.tensor, "act": nc.scalar, "dve": nc.vector, "pool": nc.gpsimd, "sp": nc.sync}
        self.esem = {}
        self.ecnt = {e: 0 for e in self.E}
        self.waited = {e: {} for e in self.E}
        self.res = {}
        self.dsem = {}
        self.nsem = 0

    def _newsem(self, name):
        self.nsem += 1
        return self.es.enter_context(self.nc.semaphore(name))

    def _wait(self, eng, tok):
        sem, val, owner, sid = tok
        if owner == "pe" and eng == "pe":
            return
        if self.waited[eng].get(sid, 0) >= val:
            return
        self.E[eng].wait_ge(sem, val)
        self.waited[eng][sid] = val

    def _deps(self, eng, reads, writes):
        toks = []
        for k in reads:
            r = self.res.get(k)
            if r and r[0]:
                toks.append(r[0])
        for k in writes:
            r = self.res.get(k)
            if r:
                if r[0]:
                    toks.append(r[0])
                toks.extend(r[1].values())
        for t in toks:
            self._wait(eng, t)

    def _commit(self, tok, reads, writes):
        for k in reads:
            self.res.setdefault(k, [None, {}])[1][tok[3]] = tok
        for k in writes:
            self.res[k] = [tok, {}]

    def op(self, eng, reads, writes, emit):
        self._deps(eng, reads, writes)
        inst = emit(self.E[eng])
        n = self.ecnt[eng]
        self.ecnt[eng] += 1
        ep = n // EPOCH
        if (eng, ep) not in self.esem:
            self.esem[(eng, ep)] = self._newsem(f"s_{eng}_{ep}")
        sem = self.esem[(eng, ep)]
        inst.then_inc(sem, 1)
        tok = (sem, n - ep * EPOCH + 1, eng, f"{eng}_{ep}")
        self._commit(tok, reads, writes)
        return tok

    def dma(self, queue, slot, pairs, reads, writes):
        self._deps(queue, reads, writes)
        if slot not in self.dsem:
            self.dsem[slot] = [self._newsem(f"d_{slot}"), 0]
        ent = self.dsem[slot]
        for (o, i) in pairs:
            self.E[queue].dma_start(out=o, in_=i).then_inc(ent[0], 16)
            ent[1] += 16
        tok = (ent[0], ent[1], "dma", f"d_{slot}")
        self._commit(tok, reads, writes)
        return tok


def build(n_layers=2):
    es = ExitStack()
    nc = bass.Bass("TRN2", target_bir_lowering=False, dynamic_dma_scratch_size=4096)
    p = Prog(nc, es)

    def dram(name, shape, kind="ExternalInput", dt=F32):
        return nc.dram_tensor(name, shape, dt, kind=kind).ap()

    xT = dram("xT", [D, T_LAT])
    ctxT = dram("ctxT", [D, T_CTX])
    par_d = dram("par", [128, NPAR])
    wada_d = dram("wada", [2, 48, 128, 1024])
    win_d = dram("win", [20, 128, 1024])
    wgate_d = dram("wgate", [40, 128, 384])
    wout_d = dram("wout", [8, 128, 1280])
    mlpin_d = dram("mlpin", [2, 32, 128, 1024])
    mlpout_d = dram("mlpout", [2, 16, 128, 2048])
    pw1_d = dram("pw1", [16, 128, 1024])
    pw2_d = dram("pw2", [8, 128, 1024])
    outT = dram("outT", [D, T_LAT], kind="ExternalOutput")

    def sb(name, shape, dt):
        return es.enter_context(nc.sbuf_tensor(name, shape, dt))

    X = sb("X", [128, NCH, T_LAT], F32)
    H2 = sb("H", [128, NCH * T_LAT], BF16)
    GU2 = sb("GU", [128, 20 * T_LAT], BF16)
    H = H2[:].rearrange("q (a t) -> q a t", a=NCH)
    GU = GU2[:].rearrange("q (a t) -> q a t", a=20)
    RING = sb("RING", [128, NSLOT * 2048], BF16)
    W16 = sb("W16", [128, 2, 2080], BF16)
    W32 = sb("W32", [128, 2, 1024], F32)
    PAR = sb("PAR", [128, NPAR], F32)
    MOD = sb("MOD", [128, 2, 2, 48], F32)
    DER = sb("DER", [128, 2, 5, 8], F32)
    SBF = sb("SBF", [128, 8, 2], BF16)
    ONES = sb("ONES", [128, 128], BF16)
    IDN = sb("IDN", [128, 128], BF16)
    GC = sb("GC", [128, 6, 20], F32)
    H0 = sb("H0", [128, 2, NR], F32)
    START = sb("START", [128, 1728], F32)
    IDXF = START[:, 0:64]
    QVF = START[:, 64:66]
    OM = START[:, 66:68]
    QT = START[:, 128:640]
    KF = START[:, 640:1152]
    KI = START[:, 640:1152].bitcast(I32)
    PE_ = START[:, 1152:1664].rearrange("q (a b) -> q a b", a=8)
    PS = es.enter_context(nc.psum_tensor("PS", [128, 8 * 512], F32))

    HF = H2[:].bitcast(F32)
    GUF = GU2[:].bitcast(F32)
    W16F = [W16[:, i, :].bitcast(F32) for i in range(2)]

    def par(name, a=0, b=None):
        o, n = _PAR[name]
        b = n if b is None else b
        return PAR[:, o + a:o + b]

    psp = [0]
    reserved = set()

    def alloc_ps(nb):
        s = psp[0]
        for _ in range(32):
            s = ((s + nb - 1) // nb) * nb
            if s + nb > 8:
                s = 0
            if not any(b in reserved for b in range(s, s + nb)):
                break
            s += nb
        else:
            raise RuntimeError("no free PSUM banks")
        psp[0] = s + nb
        return s, [f"P{b}" for b in range(s, s + nb)]

    def nbanks(T):
        return max(1, T // 512)

    wcount = [0]

    NSUB = NSLOT * 2
    wptr = [0]

    def load_w(src_ap, ncols, extra=None):
        need = 1 if ncols <= 1024 else 2
        s0 = wptr[0]
        if need == 2 and s0 % 2:
            s0 += 1
        if s0 + need > NSUB:
            s0 = 0
        wptr[0] = (s0 + need) % NSUB
        keys = [f"R{s0 + i}" for i in range(need)]
        p.dma("pool", f"R{s0}", [(RING[:, s0 * 1024:s0 * 1024 + ncols], src_ap)], [], keys + ([extra] if extra else []))
        return RING[:, s0 * 1024:(s0 + need) * 1024], keys

    def tgs(T):
        return [(a, min(T, a + 512)) for a in range(0, T, 512)]

    def mm_job(T, wt, wkey, kcs, rhs_fn, rhs_keys):
        b0, pk = alloc_ps(nbanks(T))
        psv = PS[:, b0 * 512:b0 * 512 + T]

        def emit(E):
            last = None
            for (a, b) in tgs(T):
                for idx, k in enumerate(kcs):
                    last = E.matmul(psv[:, a:b], lhsT=wt[:, idx * 128:(idx + 1) * 128], rhs=rhs_fn(k)[:, a:b],
                                    start=(idx == 0), stop=(idx == len(kcs) - 1))
            return last
        p.op("pe", wkey + rhs_keys, pk, emit)
        return psv, pk

    p.dma("sp", "par", [(PAR[:], par_d)], [], ["PAR"])
    xT_v = xT.rearrange("(c q) t -> q c t", q=128)
    XC = GUF[:, 0:2048].rearrange("q (c t) -> q c t", c=NCH)
    HC = GU2[:, 2 * 2048:3 * 2048].rearrange("q (c t) -> q c t", c=NCH)
    UC = GU2[:, 3 * 2048:3 * 2048 + NR * T_CTX].rearrange("q (c t) -> q c t", c=NR)
    ctxT_v = ctxT.rearrange("(c q) t -> q c t", q=128)
    p.dma("sp", "cin", [(XC, ctxT_v)], [], ["GU0", "GU1"])
    DUMMY = sb("DUMMY", [128, 2], F32)
    EPSC = sb("EPSC", [128, 2], F32)
    p.op("dve", [], ["EPSC"], lambda E: E.memset(EPSC[:], EPS))

    p.op("dve", [], ["ONES"], lambda E: E.memset(ONES[:], 1.0))
    p.op("pool", [], ["ST"], lambda E: E.iota(KI[:, 0:128], pattern=[[1, 128]], base=0, channel_multiplier=-1))
    p.op("dve", ["ST"], ["IDN"], lambda E: E.tensor_scalar(out=IDN[:], in0=KI[:, 0:128], scalar1=0.0, scalar2=None,
                                                           op0=ALU.is_equal))
    p.op("pool", ["IDN"], ["ST"], lambda E: E.iota(KI[:, 128:192], pattern=[[1, 64]], base=0, channel_multiplier=0))
    p.op("pool", ["ST"], ["ST"], lambda E: E.iota(KI[:, 192:194], pattern=[[128, 2]], base=0, channel_multiplier=1))
    p.op("dve", ["ST"], ["ST"], lambda E: E.tensor_copy(out=START[:, 0:66], in_=KI[:, 128:194]))
    p.op("act", ["ST"], ["ST"], lambda E: E.activation(out=OM, in_=QVF, func=AF.Exp, scale=-math.log(10000.0) / 256.0))
    p.op("dve", ["ST"], ["ST"], lambda E: E.tensor_scalar(out=OM, in0=OM, scalar1=1.0 / (2 * math.pi), scalar2=None,
                                                          op0=ALU.mult))
    for c in range(NCH):
        e = c % 2
        phase = 0.25 if (c // 2) % 2 == 1 else 0.0
        p.op("dve", ["ST"], ["ST"], lambda E: E.tensor_scalar(out=QT[:, c * 64:(c + 1) * 64], in0=IDXF, scalar1=OM[:, e:e + 1],
                                                              scalar2=phase, op0=ALU.mult, op1=ALU.add))
    p.op("dve", ["ST"], ["ST"], lambda E: E.tensor_copy(out=KI, in_=QT))
    p.op("dve", ["ST"], ["ST"], lambda E: E.tensor_copy(out=KF, in_=KI))
    p.op("dve", ["ST"], ["ST"], lambda E: E.tensor_tensor(out=QT, in0=QT, in1=KF, op=ALU.subtract))
    p.op("dve", ["ST"], ["ST"], lambda E: E.tensor_scalar(out=KF, in0=QT, scalar1=0.5, scalar2=None, op0=ALU.is_gt))
    p.op("dve", ["ST"], ["ST"], lambda E: E.tensor_tensor(out=QT, in0=QT, in1=KF, op=ALU.subtract))
    p.op("dve", ["ST"], ["ST"], lambda E: E.tensor_scalar(out=KF, in0=QT, scalar1=-0.5, scalar2=None, op0=ALU.is_lt))
    p.op("dve", ["ST"], ["ST"], lambda E: E.tensor_tensor(out=QT, in0=QT, in1=KF, op=ALU.add))
    p.op("act", ["ST"], ["ST"], lambda E: E.activation(out=START[:, 1152:1664], in_=QT, func=AF.Sin, scale=2 * math.pi))

    def pos_embed():
        for c in range(NCH):
            b = PE_[:, c, 0:32] if c < 4 else PE_[:, c, 0:64]
            if c < 4:
                bc = AP(b.tensor, b.offset, [list(b.ap[0]), [1, 32], [0, 64]])
            else:
                bc = AP(b.tensor, b.offset, [list(b.ap[0]), [0, 32], [1, 64]])
            xv = X[:, c, :].rearrange("q (r w) -> q r w", w=64)
            p.op("dve", ["ST"], [f"X{c}"], lambda E: E.tensor_tensor(out=xv, in0=xv, in1=bc, op=ALU.add))
        p.op("dve", ["ST", "IDN"], ["START", "ST"], lambda E: E.memset(DUMMY[:], 0.0))

    p.op("act", ["PAR"], ["SBF"], lambda E: E.activation(out=SBF[:].rearrange("q k t -> q (k t)"), in_=par("cvec"),
                                                         func=AF.Silu))
    CL, HCL, HBA, HBX, NHCL, GTMP = (GC[:, i, :] for i in range(6))
    p.op("act", ["PAR"], ["GC4"], lambda E: E.activation(out=GTMP, in_=par("lam"), func=AF.Exp, scale=-1.0))
    p.op("act", ["GC4"], ["GC4b"], lambda E: E.activation(out=GTMP, in_=GTMP, func=AF.Ln, bias=1.0))
    p.op("dve", ["GC4b"], ["GC0"], lambda E: E.tensor_scalar(out=CL, in0=GTMP, scalar1=-RG_C, scalar2=None, op0=ALU.mult))
    p.op("dve", ["GC0"], ["GC1"], lambda E: E.tensor_scalar(out=HCL, in0=CL, scalar1=0.5, scalar2=None, op0=ALU.mult))
    p.op("dve", ["PAR"], ["GC2"], lambda E: E.tensor_scalar(out=HBA, in0=par("ba"), scalar1=0.5, scalar2=None, op0=ALU.mult))
    p.op("dve", ["PAR"], ["GC3"], lambda E: E.tensor_scalar(out=HBX, in0=par("bx"), scalar1=0.5, scalar2=None, op0=ALU.mult))
    p.op("dve", ["GC0"], ["GC5"], lambda E: E.tensor_scalar(out=NHCL, in0=CL, scalar1=-0.5, scalar2=None, op0=ALU.mult))
    GCK = ["GC0", "GC1", "GC2", "GC3", "GC5"]

    def ada(L, j0, j1, done_key=None):
        b0, pk = alloc_ps(1)
        n = j1 - j0
        for j in range(j0, j1):
            wt, wk = load_w(wada_d[L, j], 1024, extra=(done_key if j == j1 - 1 else None))

            def emit(E, j=j, wt=wt):
                last = None
                for kc in range(8):
                    last = E.matmul(PS[:, b0 * 512 + 2 * (j - j0):b0 * 512 + 2 * (j - j0) + 2],
                                    lhsT=wt[:, kc * 128:(kc + 1) * 128], rhs=SBF[:, kc, :], start=(kc == 0), stop=(kc == 7))
                return last
            p.op("pe", wk + ["SBF"], pk, emit)
        psv = PS[:, b0 * 512:b0 * 512 + 2 * n].rearrange("q (j t) -> q t j", t=2)
        o, _ = _PAR["bada"]
        for t in range(2):
            p.op("dve", pk + ["PAR"], [f"MOD{L}_{j0}_{t}"], lambda E, t=t: E.tensor_tensor(
                out=MOD[:, L, t, j0:j1], in0=psv[:, t, :], in1=PAR[:, o + L * 48 + j0:o + L * 48 + j1], op=ALU.add))
        ks = [f"MOD{L}_{j0}_{t}" for t in range(2)]
        for j in range(j0, j1):
            modkey[(L, j)] = ks
        return ks

    modkey = {}

    def mks(L, j0, j1):
        out = []
        for j in range(j0, j1):
            for k in modkey[(L, j)]:
                if k not in out:
                    out.append(k)
        return out

    def run(gen):
        for _ in gen:
            pass

    def interleave(*gens):
        gens = list(gens)
        while gens:
            for g in list(gens):
                try:
                    next(g)
                except StopIteration:
                    gens.remove(g)

    def mod(L, t, j):
        return MOD[:, L, t, j:j + 1]

    def derive(L, which, keys):
        o, _ = _PAR["ng"]
        if which == 1:
            p.op("dve", keys + ["PAR"], [f"DER{L}_0"], lambda E: E.scalar_tensor_tensor(
                out=DER[:, L, 0, :], in0=MOD[:, L, 0, 8:16], scalar=1.0, in1=PAR[:, o + L * 16:o + L * 16 + 8],
                op0=ALU.add, op1=ALU.mult))
            p.op("dve", keys + ["PAR"], [f"DER{L}_2"], lambda E: E.scalar_tensor_tensor(
                out=DER[:, L, 2, :], in0=MOD[:, L, 1, 8:16], scalar=1.0, in1=PAR[:, o + L * 16:o + L * 16 + 8],
                op0=ALU.add, op1=ALU.mult))
        else:
            p.op("dve", keys + ["PAR"], [f"DER{L}_1"], lambda E: E.scalar_tensor_tensor(
                out=DER[:, L, 1, :], in0=MOD[:, L, 0, 32:40], scalar=1.0, in1=PAR[:, o + L * 16 + 8:o + L * 16 + 16],
                op0=ALU.add, op1=ALU.mult))

    def norm(T, src, src_keys, A, SH, dst, dst_keys, sc_keys, final=False):
        b0, pk = alloc_ps(nbanks(T))
        held = set(range(b0, b0 + nbanks(T)))
        reserved.update(held)
        psv = PS[:, b0 * 512:b0 * 512 + T]
        for c in range(NCH):
            sq = W16[:, c % 2, 0:T]
            sk = f"W16_{c % 2}"
            if c % 2 == 0 or T < 1024:
                p.op("act", [src_keys[c]], [sk], lambda E, c=c, sq=sq: E.activation(out=sq, in_=src(c), func=AF.Square))
            else:
                p.op("dve", [src_keys[c]], [sk], lambda E, c=c, sq=sq: E.tensor_tensor(out=sq, in0=src(c), in1=src(c),
                                                                                     op=ALU.mult))

            def emit(E, c=c, sq=sq):
                last = None
                for (a, b) in tgs(T):
                    last = E.matmul(psv[:, a:b], lhsT=ONES[:], rhs=sq[:, a:b], start=(c == 0), stop=(c == NCH - 1))
                return last
            p.op("pe", [sk, "ONES"], pk, emit)
            yield
        p.op("act", pk + ["EPSC"], pk, lambda E: E.activation(out=psv, in_=psv, func=AF.Ln, scale=1.0 / D, bias=EPSC[:, 0:1]))
        p.op("act", pk, pk, lambda E: E.activation(out=psv, in_=psv, func=AF.Exp, scale=-0.5))
        HT = min(T, 1024)
        i = 0
        dk = dst_keys if callable(dst_keys) else (lambda c, a: [dst_keys[c]])
        last = (NCH - 1, T - HT)
        for a in range(0, T, HT):
            for c in range(NCH):
                if final:
                    p.op("dve", [src_keys[c]] + pk + sc_keys, dk(c, a), lambda E, c=c, a=a: E.scalar_tensor_tensor(
                        out=dst(c)[:, a:a + HT], in0=src(c)[:, a:a + HT], scalar=A(c), in1=psv[:, a:a + HT],
                        op0=ALU.mult, op1=ALU.mult))
                    if (c, a) == last:
                        reserved.difference_update(held)
                    yield
                    continue
                tmp = W32[:, i % 2, 0:HT]
                tk = f"W32_{i % 2}"
                i += 1
                p.op("dve", [src_keys[c]] + pk + sc_keys, [tk], lambda E, c=c, a=a, tmp=tmp: E.scalar_tensor_tensor(
                    out=tmp, in0=src(c)[:, a:a + HT], scalar=A(c), in1=psv[:, a:a + HT], op0=ALU.mult, op1=ALU.mult))
                p.op("act", [tk] + sc_keys, dk(c, a), lambda E, c=c, a=a, tmp=tmp: E.activation(
                    out=dst(c)[:, a:a + HT], in_=tmp, func=AF.Identity, bias=SH(c), scale=1.0))
                if (c, a) == last:
                    reserved.difference_update(held)
                yield

    def rec_branch(T, hsrc, hkeys, U, Ukeys):
        o_w, _ = _PAR["rcw"]
        o_b, _ = _PAR["rcb"]
        HT = min(T, 1024)
        i = 0
        for c in range(NR):
            wt, wk = load_w(win_d[10 + c], 1024)
            psv, pk = mm_job(T, wt, wk, list(range(8)), hsrc, hkeys)
            w = [PAR[:, o_w + c * 4 + k:o_w + c * 4 + k + 1] for k in range(4)]
            bb = PAR[:, o_b + c:o_b + c + 1]
            for t0 in range(0, T, HT):
                t1 = t0 + HT
                acc = W32[:, i % 2, 0:HT]
                ak = f"W32_{i % 2}"
                i += 1
                jlo = 1 if t0 == 0 else 0

                def emitA(E, acc=acc, t0=t0, t1=t1, jlo=jlo, w=w, bb=bb, psv=psv):
                    if jlo:
                        E.activation(out=acc[:, 0:1], in_=psv[:, 0:1], func=AF.Identity, bias=bb, scale=0.0)
                    return E.activation(out=acc[:, jlo:HT], in_=psv[:, t0 + jlo - 1:t1 - 1], func=AF.Identity, bias=bb,
                                        scale=w[0])
                p.op("act", pk + ["PAR"], [ak], emitA)
                for k, off in ((2, 1), (3, 2)):
                    n = min(t1 + off, T) - (t0 + off)
                    p.op("dve", pk + [ak, "PAR"], [ak], lambda E, acc=acc, t0=t0, off=off, n=n, k=k, w=w, psv=psv:
                         E.scalar_tensor_tensor(out=acc[:, 0:n], in0=psv[:, t0 + off:t0 + off + n], scalar=w[k],
                                                in1=acc[:, 0:n], op0=ALU.mult, op1=ALU.add))
                p.op("dve", pk + [ak, "PAR"], [Ukeys[c]], lambda E, acc=acc, t0=t0, t1=t1, c=c, w=w, psv=psv:
                     E.scalar_tensor_tensor(out=U(c)[:, t0:t1], in0=psv[:, t0:t1], scalar=w[1], in1=acc, op0=ALU.mult,
                                            op1=ALU.add))

    def gate_phase(T, U, Ukeys, is_ctx, G=None, Gkeys=None):
        NH = 2 if T > 1024 else 1
        HT = T // NH
        unit = [0]
        Y0 = HF[:, 6 * 1024:6 * 1024 + T]
        Y0k = ["H6", "H7"]
        for c in range(NR):
            ks = kset(c)
            y1_tiles = {}
            for d in (0, 1):
                pss = []
                for g in (0, 1):
                    wt, wk = load_w(wgate_d[(d * 2 + g) * 10 + c], 384)
                    psv, pk = mm_job(T, wt, wk, ks, U, [Ukeys[k] for k in ks])
                    pss.append((psv, pk))
                (psR, pkR), (psI, pkI) = pss
                cl = CL[:, d * NR + c:d * NR + c + 1]
                hcl = HCL[:, d * NR + c:d * NR + c + 1]
                nhcl = NHCL[:, d * NR + c:d * NR + c + 1]
                hba = HBA[:, d * NR + c:d * NR + c + 1]
                hbx = HBX[:, d * NR + c:d * NR + c + 1]
                order = list(range(NH)) if d == 0 else list(range(NH - 1, -1, -1))
                tiles = []
                for hh in order:
                    i = unit[0] % 3
                    unit[0] += 1
                    if i < 2:
                        At = HF[:, (0 + i) * 1024:(0 + i) * 1024 + HT]
                        St = HF[:, (2 + i) * 1024:(2 + i) * 1024 + HT]
                        It = HF[:, (4 + i) * 1024:(4 + i) * 1024 + HT]
                        Ak, Sk, Ik = f"H{i}", f"H{2 + i}", f"H{4 + i}"
                    else:
                        At, St, It = W16F[0][:, 0:HT], W16F[1][:, 0:HT], START[:, 0:HT]
                        Ak, Sk, Ik = "W16_0", "W16_1", "START"
                    a0 = hh * HT
                    p.op("act", pkR + GCK, [Ak], lambda E, At=At, a0=a0, hba=hba, psR=psR: E.activation(
                        out=At, in_=psR[:, a0:a0 + HT], func=AF.Tanh, bias=hba, scale=0.5))
                    p.op("act", pkI + GCK, [Ik], lambda E, It=It, a0=a0, hbx=hbx, psI=psI: E.activation(
                        out=It, in_=psI[:, a0:a0 + HT], func=AF.Tanh, bias=hbx, scale=0.5))
                    last_half = (hh == order[-1])
                    if last_half and not is_ctx:
                        yield
                    p.op("act", [Ak] + GCK, [Sk], lambda E, At=At, St=St, cl=cl: E.activation(
                        out=St, in_=At, func=AF.Exp, bias=cl, scale=cl))
                    p.op("act", [Ak] + GCK, [Ak], lambda E, At=At, hcl=hcl: E.activation(
                        out=At, in_=At, func=AF.Exp, bias=hcl, scale=hcl))
                    tiles.append((hh, At, St, It, Ak, Sk, Ik))
                for (hh, At, St, It, Ak, Sk, Ik) in tiles:
                    p.op("act", [Sk], [Sk], lambda E, St=St: E.activation(out=St, in_=St, func=AF.Sqrt, bias=1.0,
                                                                         scale=-1.0))
                prev = None
                for (hh, At, St, It, Ak, Sk, Ik) in tiles:
                    a0 = hh * HT
                    p.op("dve", [Ik, Sk], [Ik], lambda E, It=It, St=St: E.scalar_tensor_tensor(
                        out=It, in0=It, scalar=1.0, in1=St, op0=ALU.add, op1=ALU.mult))
                    p.op("dve", [Ik, Ukeys[c]], [Ik], lambda E, It=It, a0=a0, c=c: E.scalar_tensor_tensor(
                        out=It, in0=It, scalar=0.5, in1=U(c)[:, a0:a0 + HT], op0=ALU.mult, op1=ALU.mult))
                    if d == 0:
                        yt = Y0[:, a0:a0 + HT]
                        ykeys = Y0k
                    else:
                        j = hh % 2
                        yt = W32[:, j, 0:HT]
                        ykeys = [f"W32_{j}"]
                        y1_tiles[hh] = (yt, ykeys)
                    if prev is None:
                        init = 0.0 if is_ctx else H0[:, d, c:c + 1]
                        ikeys = [] if is_ctx else ["H0"]
                    else:
                        pyt, pykeys = prev
                        init = pyt[:, HT - 1:HT] if d == 0 else pyt[:, 0:1]
                        ikeys = pykeys
                    if d == 0:
                        p.op("dve", [Ak, Ik] + ikeys, ykeys, lambda E, yt=yt, At=At, It=It, init=init:
                             E.tensor_tensor_scan(out=yt, data0=At, data1=It, initial=init, op0=ALU.mult, op1=ALU.add))
                    else:
                        p.op("dve", [Ak, Ik] + ikeys, ykeys, lambda E, yt=yt, At=At, It=It, init=init:
                             E.tensor_tensor_scan(out=yt[:, ::-1], data0=At[:, ::-1], data1=It[:, ::-1], initial=init,
                                                  op0=ALU.mult, op1=ALU.add))
                    prev = (yt, ykeys)
                if is_ctx:
                    yt, ykeys = prev
                    col = yt[:, HT - 1:HT] if d == 0 else yt[:, 0:1]
                    p.op("dve", ykeys, ["H0"], lambda E, col=col, d=d, c=c: E.tensor_copy(out=H0[:, d, c:c + 1], in_=col))
            if not is_ctx:
                for hh in range(NH - 1, -1, -1):
                    yt, ykeys = y1_tiles[hh]
                    a0 = hh * HT
                    p.op("dve", ykeys + Y0k, ykeys, lambda E, yt=yt, a0=a0: E.tensor_tensor(
                        out=yt, in0=yt, in1=Y0[:, a0:a0 + HT], op=ALU.add))
                    p.op("dve", ykeys + [Gkeys[c]], [Gkeys[c]], lambda E, yt=yt, a0=a0, c=c: E.tensor_tensor(
                        out=G(c)[:, a0:a0 + HT], in0=yt, in1=G(c)[:, a0:a0 + HT], op=ALU.mult))

    def gate_phase_ctx(U, Ukeys):
        T = T_CTX
        units = [(c, d) for d in (0, 1) for c in range(NR)]
        cA = lambda u: GUF[:, 5120 + u * 256:5120 + (u + 1) * 256]
        cS = lambda u: GUF[:, 10240 + u * 256:10240 + (u + 1) * 256]
        cI = lambda u: GUF[:, 15360 + u * 256:15360 + (u + 1) * 256]
        cTh = [GUF[:, 4352:4608], GUF[:, 4608:4864]]
        cY = GUF[:, 4864:5120]
        allk = []
        WV = 2
        for w0 in range(0, len(units), WV):
            wave = []
            for u in range(w0, w0 + WV):
                c, d = units[u]
                ks = kset(c)
                pss = []
                for g in (0, 1):
                    wt, wk = load_w(wgate_d[(d * 2 + g) * 10 + c], 384)
                    psv, pk = mm_job(T, wt, wk, ks, U, [Ukeys[k] for k in ks])
                    pss.append((psv, pk))
                sc = [t_[:, d * NR + c:d * NR + c + 1] for t_ in (CL, HCL, HBA, HBX)]
                wave.append((u, pss, sc))
                allk.extend([f"cA{u}", f"cS{u}", f"cI{u}"])
            for (u, pss, sc) in wave:
                p.op("act", pss[0][1] + GCK, [f"cA{u}"], lambda E: E.activation(out=cA(u), in_=pss[0][0], func=AF.Tanh,
                                                                                  bias=sc[2], scale=0.5))
            for (u, pss, sc) in wave:
                p.op("act", pss[1][1] + GCK, [f"cI{u}"], lambda E: E.activation(out=cI(u), in_=pss[1][0], func=AF.Tanh,
                                                                                  bias=sc[3], scale=0.5))
            for (u, pss, sc) in wave:
                p.op("act", [f"cA{u}"] + GCK, [f"cS{u}"], lambda E: E.activation(out=cS(u), in_=cA(u), func=AF.Exp,
                                                                                   bias=sc[0], scale=sc[0]))
            for (u, pss, sc) in wave:
                p.op("act", [f"cA{u}"] + GCK, [f"cA{u}"], lambda E: E.activation(out=cA(u), in_=cA(u), func=AF.Exp,
                                                                                   bias=sc[1], scale=sc[1]))
            yield
        NU = len(units)
        cAall = GUF[:, 5120:5120 + NU * 256]
        cSall = GUF[:, 10240:10240 + NU * 256]
        cIall = GUF[:, 15360:15360 + NU * 256]
        Aks = [f"cA{u}" for u in range(NU)]
        Sks = [f"cS{u}" for u in range(NU)]
        Iks = [f"cI{u}" for u in range(NU)]
        p.op("act", Sks, Sks, lambda E: E.activation(out=cSall, in_=cSall, func=AF.Sqrt, bias=1.0, scale=-1.0))
        yield
        hw = NU // 2 * 256
        p.op("dve", Aks, Aks, lambda E: E.memset(cAall[:, 0:hw:256], 0.0))
        p.op("dve", Aks, Aks, lambda E: E.memset(cAall[:, hw + 255:2 * hw:256], 0.0))
        p.op("dve", Iks + Sks, Iks, lambda E: E.scalar_tensor_tensor(out=cIall, in0=cIall, scalar=1.0, in1=cSall,
                                                                       op0=ALU.add, op1=ALU.mult))
        ucall = GU2[:, 3 * 2048:3 * 2048 + NR * T_CTX]
        for d in range(2):
            p.op("dve", Iks + ["GU3", "GU4"], Iks, lambda E: E.scalar_tensor_tensor(
                out=cIall[:, d * hw:(d + 1) * hw], in0=cIall[:, d * hw:(d + 1) * hw], scalar=0.5, in1=ucall,
                op0=ALU.mult, op1=ALU.mult))
        yield
        p.op("dve", Aks + Iks + Sks, Sks, lambda E: E.tensor_tensor_scan(
            out=cSall[:, 0:hw], data0=cAall[:, 0:hw], data1=cIall[:, 0:hw], initial=0.0, op0=ALU.mult, op1=ALU.add))
        p.op("dve", Aks + Iks + Sks, Sks, lambda E: E.tensor_tensor_scan(
            out=cSall[:, hw:2 * hw][:, ::-1], data0=cAall[:, hw:2 * hw][:, ::-1], data1=cIall[:, hw:2 * hw][:, ::-1],
            initial=0.0, op0=ALU.mult, op1=ALU.add))
        p.op("dve", Sks, ["H0"], lambda E: E.tensor_copy(out=H0[:, 0, :], in_=cSall[:, 255:hw:256]))
        p.op("dve", Sks + ["H0"], ["H0"], lambda E: E.tensor_copy(out=H0[:, 1, :], in_=cSall[:, hw:2 * hw:256]))
        yield
        p.op("dve", allk + ["cY", "cTh0", "cTh1"], [f"GU{j}" for j in range(4, 20)], lambda E: E.memset(DUMMY[:], 0.0))

    def mlp(L, gkeys):
        i = 0
        for q in range(2):
            for m in range(16):
                wt, wk = load_w(mlpin_d[L, q * 16 + m], 1024)
                for a0 in (0, 1024):
                    psv, pk = mm_job(1024, wt, wk, list(range(8)), lambda k, a0=a0: H[:, k, a0:a0 + 1024], Hhalf(a0))
                    rt = W16[:, i % 2, 0:1024]
                    rk = f"W16_{i % 2}"
                    i += 1
                    p.op("act", pk, [rk], lambda E, rt=rt, psv=psv: E.activation(out=rt, in_=psv, func=AF.Relu))
                    p.op("dve", [rk], [f"GU{m}"], lambda E, rt=rt, m=m, a0=a0: E.tensor_tensor(
                        out=GU[:, m, a0:a0 + 1024], in0=rt, in1=rt, op=ALU.mult))
            for mo in range(NCH):
                wt, wk = load_w(mlpout_d[L, q * 8 + mo], 2048)
                psv, pk = mm_job(T_LAT, wt, wk, list(range(16)), lambda k: GU[:, k, :], [f"GU{k}" for k in range(16)])
                p.op("dve", pk + gkeys + [f"X{mo}"], [f"X{mo}"], lambda E, mo=mo, psv=psv: E.scalar_tensor_tensor(
                    out=X[:, mo, :], in0=psv, scalar=mod(L, 0, 40 + mo), in1=X[:, mo, :], op0=ALU.mult, op1=ALU.add))

    Xk = [f"X{c}" for c in range(NCH)]
    Hk = [f"H{c}" for c in range(NCH)] + [f"HB{c}" for c in range(NCH)]
    hkey = lambda c, a: [f"H{c}"] if a == 0 else [f"HB{c}"]
    Hhalf = lambda a: [hkey(k, a)[0] for k in range(NCH)]

    def h_fence():
        p.op("dve", [], Hk, lambda E: E.memset(DUMMY[:], 0.0))

    L = 0
    mk_a = ["MOD0_0_0", "MOD0_0_1"]
    g_cn = norm(T_CTX, lambda c: XC[:, c, :], ["GU0"] * 4 + ["GU1"] * 4, lambda c: DER[:, 0, 2, c:c + 1],
                lambda c: mod(0, 1, c), lambda c: HC[:, c, :], ["GU2"] * 8, mk_a + ["DER0_2"])
    for _ in range(NCH):
        next(g_cn)
    ada(0, 0, 16, done_key="ADA0_DONE")
    assert mks(0, 0, 16) == mk_a
    for c in range(NCH):
        p.dma("sp", f"xin{c}", [(X[:, c, :], xT_v[:, c, :])], ["ADA0_DONE"], [f"X{c}"])
    derive(0, 1, mk_a)
    run(g_cn)
    UCk = ["GU3"] * 8 + ["GU4"] * 2
    rec_branch(T_CTX, lambda k: HC[:, k, :], ["GU2"], lambda c: UC[:, c, :], UCk)
    pos_embed()
    g_cg = gate_phase_ctx(lambda c: UC[:, c, :], UCk)
    g_ln = norm(T_LAT, lambda c: X[:, c, :], Xk, lambda c: DER[:, 0, 0, c:c + 1], lambda c: mod(0, 0, c),
                lambda c: H[:, c, :], hkey, mk_a + ["DER0_0"])
    for _ in range(3):
        next(g_cg)
    for _ in g_cg:
        for _ in range(3):
            next(g_ln, None)
    run(g_ln)
    Uk = [f"GU{10 + c}" for c in range(NR)]
    Gk = [f"GU{c}" for c in range(NR)]
    rec_branch(T_LAT, lambda k: H[:, k, :], Hk, lambda c: GU[:, 10 + c, :], Uk)
    for c in range(NR):
        wt, wk = load_w(win_d[c], 1024)
        psv, pk = mm_job(T_LAT, wt, wk, list(range(8)), lambda k: H[:, k, :], Hk)
        p.op("act", pk, [Gk[c]], lambda E, c=c, psv=psv: E.activation(out=GU[:, c, :], in_=psv, func=AF.Gelu_apprx_tanh))
    groups = [(0, j, j + 4) for j in range(16, 48, 4)]
    if n_layers > 1:
        groups += [(1, j, j + 4) for j in range(0, 48, 4)]
    gi_ = iter(groups)
    h_fence()
    for _ in gate_phase(T_LAT, lambda c: GU[:, 10 + c, :], Uk, False, lambda c: GU[:, c, :], Gk):
        g_ = next(gi_, None)
        if g_ is not None:
            ada(*g_)
    for g_ in gi_:
        ada(*g_)
    h_fence()
    mk_b = mks(0, 16, 48)
    derive(0, 2, mk_b)
    if n_layers > 1:
        mk1_a = mks(1, 0, 24)
        mk1_b = mks(1, 24, 48)
    for mo in range(NCH):
        wt, wk = load_w(wout_d[mo], 1280)
        psv, pk = mm_job(T_LAT, wt, wk, list(range(NR)), lambda k: GU[:, k, :], Gk)
        p.op("dve", pk + mk_b + [f"X{mo}"], [f"X{mo}"], lambda E, mo=mo, psv=psv: E.scalar_tensor_tensor(
            out=X[:, mo, :], in0=psv, scalar=mod(0, 0, 16 + mo), in1=X[:, mo, :], op0=ALU.mult, op1=ALU.add))
    run(norm(T_LAT, lambda c: X[:, c, :], Xk, lambda c: DER[:, 0, 1, c:c + 1], lambda c: mod(0, 0, 24 + c),
             lambda c: H[:, c, :], hkey, mk_b + ["DER0_1"]))
    mlp(0, mk_b)

    if n_layers > 1:
        L = 1
        derive(1, 1, mk1_a)
        derive(1, 2, mk1_b)
        o_b2, _ = _PAR["bpw2"]
        p.op("dve", mk1_a + ["PAR"], ["DER1_3"], lambda E: E.tensor_tensor(
            out=DER[:, 1, 3, :], in0=MOD[:, 1, 0, 16:24], in1=PAR[:, o_b2:o_b2 + 8], op=ALU.mult))
        run(norm(T_LAT, lambda c: X[:, c, :], Xk, lambda c: DER[:, 1, 0, c:c + 1], lambda c: mod(1, 0, c),
                 lambda c: H[:, c, :], hkey, mk1_a + ["DER1_0"]))
        for i in range(2):
            def emit(E, i=i):
                E.memset(W16[:, i, 0:15], 0.0)
                return E.memset(W16[:, i, 15 + T_LAT:2080], 0.0)
            p.op("dve", [], [f"W16_{i}", f"Z{i}a", f"Z{i}b"], emit)
        DGB = [GU2[:, (16 + 2 * b) * 2048:(16 + 2 * b) * 2048 + KW * 128].rearrange("q (k m) -> q k m", k=KW) for b in range(2)]
        o_cw, _ = _PAR["ccw"]
        o_cb, _ = _PAR["ccb"]
        o_b1, _ = _PAR["bpw1"]
        ZC = GUF[:, 0:16 * 1024].rearrange("q (c t) -> q c t", c=NCH)
        gi = 0
        for c in range(NCH):
            db = c % 2
            dkeys = [f"GU{16 + 2 * db}", f"GU{17 + 2 * db}"]

            def emit(E, c=c, db=db):
                last = None
                for k in range(NPE):
                    last = E.tensor_scalar(out=DGB[db][:, k, :], in0=IDN[:], scalar1=PAR[:, o_cw + c * KW + k:o_cw + c * KW + k + 1],
                                           scalar2=None, op0=ALU.mult)
                return last
            p.op("dve", ["IDN", "PAR"], dkeys, emit)
            wtA, wkA = load_w(pw1_d[2 * c], 1024)
            wtB, wkB = load_w(pw1_d[2 * c + 1], 1024)
            zt = W16[:, c % 2, :]
            zka, zkb = f"Z{c % 2}a", f"Z{c % 2}b"
            for a0 in (0, 1024):
                psA, pkA = mm_job(1024, wtA, wkA, list(range(8)), lambda k, a0=a0: H[:, k, a0:a0 + 1024], Hhalf(a0))
                psB, pkB = mm_job(1024, wtB, wkB, list(range(8)), lambda k, a0=a0: H[:, k, a0:a0 + 1024], Hhalf(a0))
                sg = W32[:, gi % 2, :]
                sgk = f"W32_{gi % 2}"
                gi += 1
                p.op("act", pkB + ["PAR"], [sgk], lambda E, sg=sg, c=c, psB=psB: E.activation(
                    out=sg, in_=psB, func=AF.Sigmoid, bias=PAR[:, o_b1 + 8 + c:o_b1 + 9 + c], scale=1.0))
                p.op("dve", pkA + [sgk, "PAR"], [zka if a0 == 0 else zkb],
                     lambda E, sg=sg, a0=a0, c=c, psA=psA, zt=zt: E.scalar_tensor_tensor(
                         out=zt[:, 15 + a0:15 + a0 + 1024], in0=psA, scalar=PAR[:, o_b1 + c:o_b1 + c + 1],
                         in1=sg, op0=ALU.add, op1=ALU.mult))
            zkeys_tg = [[zka], [zka, zkb], [zka, zkb], [zkb]]
            for tg in range(4):
                b0, pk = alloc_ps(1)
                pv = PS[:, b0 * 512:(b0 + 1) * 512]
                bd, pkd = alloc_ps(1)
                pd = PS[:, bd * 512:(bd + 1) * 512]
                zck = f"GU{2 * c + tg // 2}"
                zcv = ZC[:, c, tg * 512:(tg + 1) * 512]

                def emit(E, tg=tg, db=db, pv=pv, zt=zt):
                    last = None
                    for k in range(NPE):
                        last = E.matmul(pv, lhsT=DGB[db][:, k, :], rhs=zt[:, tg * 512 + k:tg * 512 + k + 512],
                                        start=(k == 0), stop=(k == NPE - 1))
                    return last
                p.op("pe", dkeys + zkeys_tg[tg], pk, emit)

                tapsA = list(range(NPE, KW, 2))
                tapsB = list(range(NPE + 1, KW, 2))
                for i in range(max(len(tapsA), len(tapsB))):
                    if i < len(tapsA):
                        k = tapsA[i]
                        wk_ = PAR[:, o_cw + c * KW + k:o_cw + c * KW + k + 1]
                        src = zt[:, tg * 512 + k:tg * 512 + k + 512]
                        if i == 0:
                            p.op("dve", zkeys_tg[tg] + ["PAR"], pkd, lambda E: E.tensor_scalar(
                                out=pd, in0=src, scalar1=wk_, scalar2=PAR[:, o_cb + c:o_cb + c + 1], op0=ALU.mult, op1=ALU.add))
                        else:
                            p.op("dve", zkeys_tg[tg] + ["PAR"] + pkd, pkd, lambda E: E.scalar_tensor_tensor(
                                out=pd, in0=src, scalar=wk_, in1=pd, op0=ALU.mult, op1=ALU.add))
                    if i < len(tapsB):
                        k = tapsB[i]
                        wk_ = PAR[:, o_cw + c * KW + k:o_cw + c * KW + k + 1]
                        src = zt[:, tg * 512 + k:tg * 512 + k + 512]
                        if i == 0:
                            p.op("dve", zkeys_tg[tg] + ["PAR"], [zck], lambda E: E.tensor_scalar(
                                out=zcv, in0=src, scalar1=wk_, scalar2=None, op0=ALU.mult))
                        else:
                            p.op("dve", zkeys_tg[tg] + ["PAR", zck], [zck], lambda E: E.scalar_tensor_tensor(
                                out=zcv, in0=src, scalar=wk_, in1=zcv, op0=ALU.mult, op1=ALU.add))
                p.op("dve", pkd + [zck], [zck], lambda E: E.tensor_tensor(out=zcv, in0=zcv, in1=pd, op=ALU.add))
                p.op("dve", pk + [zck], [zck], lambda E, pv=pv, zcv=zcv: E.tensor_tensor(out=zcv, in0=zcv, in1=pv, op=ALU.add))
        p.op("dve", [], ["W16_0", "W16_1", "Z0a", "Z0b", "Z1a", "Z1b"], lambda E: E.memset(DUMMY[:], 0.0))
        o_g, _ = _PAR["lng"]
        o_lb, _ = _PAR["lnb"]
        st = []
        for hf in range(2):
            a0 = hf * 1024
            b1, pk1 = alloc_ps(2)
            b2, pk2 = alloc_ps(2)
            S1 = PS[:, b1 * 512:b1 * 512 + 1024]
            S2 = PS[:, b2 * 512:b2 * 512 + 1024]
            st.append((a0, pk1, pk2, S1, S2))
            for c in range(NCH):
                zk = f"GU{2 * c + hf}"
                tb = W16[:, c % 2, 0:1024]
                tq = W16[:, c % 2, 1024:2048]
                tk = f"W16_{c % 2}"
                p.op("act", [zk], [tk], lambda E, c=c, tb=tb: E.activation(out=tb, in_=ZC[:, c, a0:a0 + 1024],
                                                                               func=AF.Identity))
                p.op("act", [zk, tk], [tk], lambda E, c=c, tq=tq: E.activation(out=tq, in_=ZC[:, c, a0:a0 + 1024],
                                                                               func=AF.Square))

                def emit(E, c=c, tb=tb, tq=tq):
                    last = None
                    for t in range(2):
                        E.matmul(S1[:, t * 512:(t + 1) * 512], lhsT=ONES[:], rhs=tb[:, t * 512:(t + 1) * 512],
                                 start=(c == 0), stop=(c == NCH - 1))
                        last = E.matmul(S2[:, t * 512:(t + 1) * 512], lhsT=ONES[:], rhs=tq[:, t * 512:(t + 1) * 512],
                                        start=(c == 0), stop=(c == NCH - 1))
                    return last
                p.op("pe", [tk, "ONES"], pk1 + pk2, emit)
        for hf in range(2):
            a0, pk1, pk2, S1, S2 = st[hf]
            mean = W32[:, 0, :]
            var = W32[:, 1, :]
            p.op("act", pk1, ["W32_0"], lambda E: E.activation(out=mean, in_=S1, func=AF.Identity, scale=1.0 / D))
            p.op("dve", ["W32_0"], ["W32_1"], lambda E: E.tensor_tensor(out=var, in0=mean, in1=mean, op=ALU.mult))
            p.op("dve", pk2 + ["W32_1"], ["W32_1"], lambda E: E.scalar_tensor_tensor(
                out=var, in0=S2, scalar=1.0 / D, in1=var, op0=ALU.mult, op1=ALU.subtract))
            p.op("act", ["W32_1", "EPSC"], ["W32_1"], lambda E: E.activation(out=var, in_=var, func=AF.Ln, bias=EPSC[:, 0:1], scale=1.0))
            p.op("act", ["W32_1"], pk2, lambda E: E.activation(out=S2, in_=var, func=AF.Exp, scale=-0.5))
            p.op("dve", ["W32_0"] + pk2, pk1, lambda E: E.scalar_tensor_tensor(
                out=S1, in0=mean, scalar=-1.0, in1=S2, op0=ALU.mult, op1=ALU.mult))
        for hf in range(2):
            a0, pk1, pk2, S1, S2 = st[hf]
            for c in range(NCH):
                zk = f"GU{2 * c + hf}"
                zz = ZC[:, c, a0:a0 + 1024]
                p.op("dve", [zk] + pk2, [zk], lambda E, zz=zz: E.tensor_tensor(out=zz, in0=zz, in1=S2, op=ALU.mult))
                p.op("dve", [zk] + pk1, [zk], lambda E, zz=zz: E.tensor_tensor(out=zz, in0=zz, in1=S1, op=ALU.add))
                p.op("act", [zk, "PAR"], hkey(c, a0), lambda E, zz=zz, c=c: E.activation(
                    out=H[:, c, a0:a0 + 1024], in_=zz, func=AF.Silu, bias=PAR[:, o_lb + c:o_lb + c + 1],
                    scale=PAR[:, o_g + c:o_g + c + 1]))
        for mo in range(NCH):
            p.op("act", ["DER1_3", f"X{mo}"], [f"X{mo}"], lambda E, mo=mo: E.activation(
                out=X[:, mo, :], in_=X[:, mo, :], func=AF.Identity, bias=DER[:, 1, 3, mo:mo + 1], scale=1.0))
        for a0 in (0, 1024):
            for mo in range(NCH):
                wt, wk = load_w(pw2_d[mo], 1024)
                psv, pk = mm_job(1024, wt, wk, list(range(8)), lambda k, a0=a0: H[:, k, a0:a0 + 1024], Hhalf(a0))
                p.op("dve", pk + mk1_a + [f"X{mo}"], [f"X{mo}"], lambda E, mo=mo, psv=psv, a0=a0: E.scalar_tensor_tensor(
                    out=X[:, mo, a0:a0 + 1024], in0=psv, scalar=mod(1, 0, 16 + mo), in1=X[:, mo, a0:a0 + 1024],
                    op0=ALU.mult, op1=ALU.add))
        run(norm(T_LAT, lambda c: X[:, c, :], Xk, lambda c: DER[:, 1, 1, c:c + 1], lambda c: mod(1, 0, 24 + c),
                 lambda c: H[:, c, :], hkey, mk1_b + ["DER1_1"]))
        mlp(1, mk1_b)

    o_fg, _ = _PAR["fg"]
    run(norm(T_LAT, lambda c: X[:, c, :], Xk, lambda c: PAR[:, o_fg + c:o_fg + c + 1], None,
             lambda c: X[:, c, :], Xk, ["PAR"], final=True))
    outT_v = outT.rearrange("(c q) t -> q c t", q=128)
    toks = [p.dma("sp", f"out{c}", [(outT_v[:, c, :], X[:, c, :])], [Xk[c]], []) for c in range(NCH)]
    for tok in toks:
        nc.sync.wait_ge(tok[0], tok[1])
    return nc, es


def _fm(v, nchunk):
    return np.ascontiguousarray(np.asarray(v, np.float32).reshape(nchunk, 128).T)


def _tile(W, rows, cols):
    sub = W[rows][:, cols]
    kc = sub.shape[0] // 128
    return np.ascontiguousarray(sub.reshape(kc, 128, sub.shape[1]).transpose(1, 0, 2).reshape(128, kc * sub.shape[1]))


_NC_CACHE = {}


def _get_nc(n_layers):
    if n_layers not in _NC_CACHE:
        _NC_CACHE[n_layers] = build(n_layers)
    return _NC_CACHE[n_layers][0]


def prep_inputs(x, c, ctx, c_ctx, w_ada, b_ada, norm_g, rec_w_in, rec_conv_w, rec_conv_b, rec_lambda,
                rec_w_a, rec_b_a, rec_w_x, rec_b_x, rec_w_out, conf_w_pw1, conf_b_pw1, conf_conv_w, conf_conv_b,
                conf_ln_g, conf_ln_b, conf_w_pw2, conf_b_pw2, mlp_w_in, mlp_w_out, final_g):
    f = lambda a: np.asarray(a, np.float32)
    x, c, ctx, c_ctx = f(x), f(c), f(ctx), f(c_ctx)
    w_ada, mlp_w_in, mlp_w_out = f(w_ada), f(mlp_w_in), f(mlp_w_out)
    sl = lambda i: slice(i * 128, (i + 1) * 128)
    allk = lambda K: slice(0, K)
    wada = np.stack([np.stack([_tile(w_ada[L], allk(D), sl(j)) for j in range(48)]) for L in range(2)])
    win = np.stack([_tile(f(rec_w_in)[0], allk(D), sl(m)) for m in range(20)])
    wgate = np.zeros((40, 128, 384), np.float32)
    for d in range(2):
        for g, w in enumerate((f(rec_w_a)[0, d], f(rec_w_x)[0, d])):
            Wbd = np.zeros((R, R), np.float32)
            for h in range(16):
                Wbd[h * 80:(h + 1) * 80, h * 80:(h + 1) * 80] = w[h]
            for cc in range(NR):
                ks = kset(cc)
                for idx, k in enumerate(ks):
                    wgate[(d * 2 + g) * 10 + cc, :, idx * 128:(idx + 1) * 128] = Wbd[sl(k), sl(cc)]
    wout = np.stack([_tile(f(rec_w_out)[0], allk(R), sl(m)) for m in range(8)])
    mlpin = np.stack([np.stack([_tile(mlp_w_in[L], allk(D), sl(m)) for m in range(32)]) for L in range(2)])
    mlpout = np.stack([np.stack([_tile(mlp_w_out[L], slice(q * 2048, (q + 1) * 2048), sl(mo))
                                 for q in range(2) for mo in range(8)]) for L in range(2)])
    pw1 = np.stack([_tile(f(conf_w_pw1)[0], allk(D), sl(m)) for c8 in range(8) for m in (c8, 8 + c8)])
    pw2 = np.stack([_tile(f(conf_w_pw2)[0], allk(D), sl(m)) for m in range(8)])

    shared = dict(wada=wada, win=win, wgate=wgate, wout=wout, mlpin=mlpin, mlpout=mlpout, pw1=pw1, pw2=pw2)
    pbase = np.zeros((128, NPAR), np.float32)

    def put(name, arr):
        o, n = _PAR[name]
        pbase[:, o:o + n] = np.asarray(arr, np.float32).reshape(128, n)
    put("bada", np.stack([_fm(f(b_ada)[L], 48) for L in range(2)], axis=1))
    put("ng", np.stack([_fm(f(norm_g)[L, n], 8) for L in range(2) for n in range(2)], axis=1))
    put("fg", _fm(final_g, 8))
    put("rcw", np.stack([_fm(f(rec_conv_w)[0, k], NR) for k in range(4)], axis=2))
    put("rcb", _fm(f(rec_conv_b)[0], NR))
    put("lam", np.stack([_fm(f(rec_lambda)[0, d], NR) for d in range(2)], axis=1))
    put("ba", np.stack([_fm(f(rec_b_a)[0, d].reshape(-1), NR) for d in range(2)], axis=1))
    put("bx", np.stack([_fm(f(rec_b_x)[0, d].reshape(-1), NR) for d in range(2)], axis=1))
    put("bpw1", _fm(f(conf_b_pw1)[0], 16))
    put("ccw", np.stack([_fm(f(conf_conv_w)[0, k], 8) for k in range(KW)], axis=2))
    put("ccb", _fm(f(conf_conv_b)[0], 8))
    put("lng", _fm(f(conf_ln_g)[0], 8))
    put("lnb", _fm(f(conf_ln_b)[0], 8))
    put("bpw2", _fm(f(conf_b_pw2)[0], 8))
    in_maps = []
    for b in range(8):
        pb = pbase.copy()
        o, n = _PAR["cvec"]
        pb[:, o:o + n] = np.stack([_fm(c[b], 8), _fm(c_ctx, 8)], axis=2).reshape(128, 16)
        m = dict(shared)
        m["xT"] = np.ascontiguousarray(x[b].T)
        m["ctxT"] = np.ascontiguousarray(ctx[b].T)
        m["par"] = pb
        in_maps.append(m)
    return in_maps


def run(inputs, n_layers=2, trace=False):
    nc = _get_nc(n_layers)
    in_maps = prep_inputs(**inputs)
    res = run_bass_kernel_spmd(nc, in_maps, core_ids=list(range(8)), trace=trace)
    out = np.stack([np.ascontiguousarray(r["outT"].T) for r in res.results]).astype(np.float32)
    return out, res


def kernel(**inputs):
    out, _ = run(inputs, 2)
    return out
```

```python
import math
from contextlib import ExitStack

import numpy as np
import concourse.bass as bass
import concourse.mybir as mybir
from concourse.ap import AP
from concourse.bass_utils import run_bass_kernel_spmd

F32 = mybir.dt.float32
BF16 = mybir.dt.bfloat16
I32 = mybir.dt.int32
AF = mybir.ActivationFunctionType
ALU = mybir.AluOpType

D = 1024
T_LAT = 2048
T_CTX = 256
R = 1280
NCH = 8
NR = 10
DFF = 4096
KW = 31
EPS = 1e-6
RG_C = 8.0
NSLOT = 4
NPE = 23
EPOCH = 3000


def kset(c):
    b0 = (128 * c) // 80
    b1 = (128 * c + 127) // 80
    k0 = (80 * b0) // 128
    k1 = (80 * (b1 + 1) - 1) // 128
    return list(range(k0, k1 + 1))


_PAR = {}
_off = 0
for _name, _n in [("cvec", 16), ("bada", 96), ("ng", 32), ("fg", 8), ("rcw", 40), ("rcb", 10),
                  ("lam", 20), ("ba", 20), ("bx", 20), ("bpw1", 16), ("ccw", 8 * KW), ("ccb", 8),
                  ("lng", 8), ("lnb", 8), ("bpw2", 8)]:
    _PAR[_name] = (_off, _n)
    _off += _n
NPAR = _off


class Prog:
    def __init__(self, nc, es):
        self.nc = nc
        self.es = es
        self.E = {"pe": nc.tensor, "act": nc.scalar, "dve": nc.vector, "pool": nc.gpsimd, "sp": nc.sync}
        self.esem = {}
        self.ecnt = {e: 0 for e in self.E}
        self.waited = {e: {} for e in self.E}
        self.res = {}
        self.dsem = {}
        self.nsem = 0

    def _newsem(self, name):
        self.nsem += 1
        return self.es.enter_context(self.nc.semaphore(name))

    def _wait(self, eng, tok):
        sem, val, owner, sid = tok
        if owner == "pe" and eng == "pe":
            return
        if self.waited[eng].get(sid, 0) >= val:
            return
        self.E[eng].wait_ge(sem, val)
        self.waited[eng][sid] = val

    def _deps(self, eng, reads, writes):
        toks = []
        for k in reads:
            r = self.res.get(k)
            if r and r[0]:
                toks.append(r[0])
        for k in writes:
            r = self.res.get(k)
            if r:
                if r[0]:
                    toks.append(r[0])
                toks.extend(r[1].values())
        for t in toks:
            self._wait(eng, t)

    def _commit(self, tok, reads, writes):
        for k in reads:
            self.res.setdefault(k, [None, {}])[1][tok[3]] = tok
        for k in writes:
            self.res[k] = [tok, {}]

    def op(self, eng, reads, writes, emit):
        self._deps(eng, reads, writes)
        inst = emit(self.E[eng])
        n = self.ecnt[eng]
        self.ecnt[eng] += 1
        ep = n // EPOCH
        if (eng, ep) not in self.esem:
            self.esem[(eng, ep)] = self._newsem(f"s_{eng}_{ep}")
        sem = self.esem[(eng, ep)]
        inst.then_inc(sem, 1)
        tok = (sem, n - ep * EPOCH + 1, eng, f"{eng}_{ep}")
        self._commit(tok, reads, writes)
        return tok

    def dma(self, queue, slot, pairs, reads, writes):
        self._deps(queue, reads, writes)
        if slot not in self.dsem:
            self.dsem[slot] = [self._newsem(f"d_{slot}"), 0]
        ent = self.dsem[slot]
        for (o, i) in pairs:
            self.E[queue].dma_start(out=o, in_=i).then_inc(ent[0], 16)
            ent[1] += 16
        tok = (ent[0], ent[1], "dma", f"d_{slot}")
        self._commit(tok, reads, writes)
        return tok


def build(n_layers=2):
    es = ExitStack()
    nc = bass.Bass("TRN2", target_bir_lowering=False, dynamic_dma_scratch_size=4096)
    p = Prog(nc, es)

    def dram(name, shape, kind="ExternalInput", dt=F32):
        return nc.dram_tensor(name, shape, dt, kind=kind).ap()

    xT = dram("xT", [D, T_LAT])
    ctxT = dram("ctxT", [D, T_CTX])
    par_d = dram("par", [128, NPAR])
    wada_d = dram("wada", [2, 48, 128, 1024])
    win_d = dram("win", [20, 128, 1024])
    wgate_d = dram("wgate", [40, 128, 384])
    wout_d = dram("wout", [8, 128, 1280])
    mlpin_d = dram("mlpin", [2, 32, 128, 1024])
    mlpout_d = dram("mlpout", [2, 16, 128, 2048])
    pw1_d = dram("pw1", [16, 128, 1024])
    pw2_d = dram("pw2", [8, 128, 1024])
    outT = dram("outT", [D, T_LAT], kind="ExternalOutput")

    def sb(name, shape, dt):
        return es.enter_context(nc.sbuf_tensor(name, shape, dt))

    X = sb("X", [128, NCH, T_LAT], F32)
    H2 = sb("H", [128, NCH * T_LAT], BF16)
    GU2 = sb("GU", [128, 20 * T_LAT], BF16)
    H = H2[:].rearrange("q (a t) -> q a t", a=NCH)
    GU = GU2[:].rearrange("q (a t) -> q a t", a=20)
    RING = sb("RING", [128, NSLOT * 2048], BF16)
    W16 = sb("W16", [128, 2, 2080], BF16)
    W32 = sb("W32", [128, 2, 1024], F32)
    PAR = sb("PAR", [128, NPAR], F32)
    MOD = sb("MOD", [128, 2, 2, 48], F32)
    DER = sb("DER", [128, 2, 5, 8], F32)
    SBF = sb("SBF", [128, 8, 2], BF16)
    ONES = sb("ONES", [128, 128], BF16)
    IDN = sb("IDN", [128, 128], BF16)
    GC = sb("GC", [128, 6, 20], F32)
    H0 = sb("H0", [128, 2, NR], F32)
    START = sb("START", [128, 1728], F32)
    IDXF = START[:, 0:64]
    QVF = START[:, 64:66]
    OM = START[:, 66:68]
    QT = START[:, 128:640]
    KF = START[:, 640:1152]
    KI = START[:, 640:1152].bitcast(I32)
    PE_ = START[:, 1152:1664].rearrange("q (a b) -> q a b", a=8)
    PS = es.enter_context(nc.psum_tensor("PS", [128, 8 * 512], F32))

    HF = H2[:].bitcast(F32)
    GUF = GU2[:].bitcast(F32)
    W16F = [W16[:, i, :].bitcast(F32) for i in range(2)]

    def par(name, a=0, b=None):
        o, n = _PAR[name]
        b = n if b is None else b
        return PAR[:, o + a:o + b]

    psp = [0]
    reserved = set()

    def alloc_ps(nb):
        s = psp[0]
        for _ in range(32):
            s = ((s + nb - 1) // nb) * nb
            if s + nb > 8:
                s = 0
            if not any(b in reserved for b in range(s, s + nb)):
                break
            s += nb
        else:
            raise RuntimeError("no free PSUM banks")
        psp[0] = s + nb
        return s, [f"P{b}" for b in range(s, s + nb)]

    def nbanks(T):
        return max(1, T // 512)

    wcount = [0]

    NSUB = NSLOT * 2
    wptr = [0]

    def load_w(src_ap, ncols, extra=None):
        need = 1 if ncols <= 1024 else 2
        s0 = wptr[0]
        if need == 2 and s0 % 2:
            s0 += 1
        if s0 + need > NSUB:
            s0 = 0
        wptr[0] = (s0 + need) % NSUB
        keys = [f"R{s0 + i}" for i in range(need)]
        p.dma("pool", f"R{s0}", [(RING[:, s0 * 1024:s0 * 1024 + ncols], src_ap)], [], keys + ([extra] if extra else []))
        return RING[:, s0 * 1024:(s0 + need) * 1024], keys

    def tgs(T):
        return [(a, min(T, a + 512)) for a in range(0, T, 512)]

    def mm_job(T, wt, wkey, kcs, rhs_fn, rhs_keys):
        b0, pk = alloc_ps(nbanks(T))
        psv = PS[:, b0 * 512:b0 * 512 + T]

        def emit(E):
            last = None
            for (a, b) in tgs(T):
                for idx, k in enumerate(kcs):
                    last = E.matmul(psv[:, a:b], lhsT=wt[:, idx * 128:(idx + 1) * 128], rhs=rhs_fn(k)[:, a:b],
                                    start=(idx == 0), stop=(idx == len(kcs) - 1))
            return last
        p.op("pe", wkey + rhs_keys, pk, emit)
        return psv, pk

    p.dma("sp", "par", [(PAR[:], par_d)], [], ["PAR"])
    xT_v = xT.rearrange("(c q) t -> q c t", q=128)
    XC = GUF[:, 0:2048].rearrange("q (c t) -> q c t", c=NCH)
    HC = GU2[:, 2 * 2048:3 * 2048].rearrange("q (c t) -> q c t", c=NCH)
    UC = GU2[:, 3 * 2048:3 * 2048 + NR * T_CTX].rearrange("q (c t) -> q c t", c=NR)
    ctxT_v = ctxT.rearrange("(c q) t -> q c t", q=128)
    p.dma("sp", "cin", [(XC, ctxT_v)], [], ["GU0", "GU1"])
    DUMMY = sb("DUMMY", [128, 2], F32)
    EPSC = sb("EPSC", [128, 2], F32)
    p.op("dve", [], ["EPSC"], lambda E: E.memset(EPSC[:], EPS))

    p.op("dve", [], ["ONES"], lambda E: E.memset(ONES[:], 1.0))
    p.op("pool", [], ["ST"], lambda E: E.iota(KI[:, 0:128], pattern=[[1, 128]], base=0, channel_multiplier=-1))
    p.op("dve", ["ST"], ["IDN"], lambda E: E.tensor_scalar(out=IDN[:], in0=KI[:, 0:128], scalar1=0.0, scalar2=None,
                                                           op0=ALU.is_equal))
    p.op("pool", ["IDN"], ["ST"], lambda E: E.iota(KI[:, 128:192], pattern=[[1, 64]], base=0, channel_multiplier=0))
    p.op("pool", ["ST"], ["ST"], lambda E: E.iota(KI[:, 192:194], pattern=[[128, 2]], base=0, channel_multiplier=1))
    p.op("dve", ["ST"], ["ST"], lambda E: E.tensor_copy(out=START[:, 0:66], in_=KI[:, 128:194]))
    p.op("act", ["ST"], ["ST"], lambda E: E.activation(out=OM, in_=QVF, func=AF.Exp, scale=-math.log(10000.0) / 256.0))
    p.op("dve", ["ST"], ["ST"], lambda E: E.tensor_scalar(out=OM, in0=OM, scalar1=1.0 / (2 * math.pi), scalar2=None,
                                                          op0=ALU.mult))
    for c in range(NCH):
        e = c % 2
        phase = 0.25 if (c // 2) % 2 == 1 else 0.0
        p.op("dve", ["ST"], ["ST"], lambda E: E.tensor_scalar(out=QT[:, c * 64:(c + 1) * 64], in0=IDXF, scalar1=OM[:, e:e + 1],
                                                              scalar2=phase, op0=ALU.mult, op1=ALU.add))
    p.op("dve", ["ST"], ["ST"], lambda E: E.tensor_copy(out=KI, in_=QT))
    p.op("dve", ["ST"], ["ST"], lambda E: E.tensor_copy(out=KF, in_=KI))
    p.op("dve", ["ST"], ["ST"], lambda E: E.tensor_tensor(out=QT, in0=QT, in1=KF, op=ALU.subtract))
    p.op("dve", ["ST"], ["ST"], lambda E: E.tensor_scalar(out=KF, in0=QT, scalar1=0.5, scalar2=None, op0=ALU.is_gt))
    p.op("dve", ["ST"], ["ST"], lambda E: E.tensor_tensor(out=QT, in0=QT, in1=KF, op=ALU.subtract))
    p.op("dve", ["ST"], ["ST"], lambda E: E.tensor_scalar(out=KF, in0=QT, scalar1=-0.5, scalar2=None, op0=ALU.is_lt))
    p.op("dve", ["ST"], ["ST"], lambda E: E.tensor_tensor(out=QT, in0=QT, in1=KF, op=ALU.add))
    p.op("act", ["ST"], ["ST"], lambda E: E.activation(out=START[:, 1152:1664], in_=QT, func=AF.Sin, scale=2 * math.pi))

    def pos_embed():
        for c in range(NCH):
            b = PE_[:, c, 0:32] if c < 4 else PE_[:, c, 0:64]
            if c < 4:
                bc = AP(b.tensor, b.offset, [list(b.ap[0]), [1, 32], [0, 64]])
            else:
                bc = AP(b.tensor, b.offset, [list(b.ap[0]), [0, 32], [1, 64]])
            xv = X[:, c, :].rearrange("q (r w) -> q r w", w=64)
            p.op("dve", ["ST"], [f"X{c}"], lambda E: E.tensor_tensor(out=xv, in0=xv, in1=bc, op=ALU.add))
        p.op("dve", ["ST", "IDN"], ["START", "ST"], lambda E: E.memset(DUMMY[:], 0.0))

    p.op("act", ["PAR"], ["SBF"], lambda E: E.activation(out=SBF[:].rearrange("q k t -> q (k t)"), in_=par("cvec"),
                                                         func=AF.Silu))
    CL, HCL, HBA, HBX, NHCL, GTMP = (GC[:, i, :] for i in range(6))
    p.op("act", ["PAR"], ["GC4"], lambda E: E.activation(out=GTMP, in_=par("lam"), func=AF.Exp, scale=-1.0))
    p.op("act", ["GC4"], ["GC4b"], lambda E: E.activation(out=GTMP, in_=GTMP, func=AF.Ln, bias=1.0))
    p.op("dve", ["GC4b"], ["GC0"], lambda E: E.tensor_scalar(out=CL, in0=GTMP, scalar1=-RG_C, scalar2=None, op0=ALU.mult))
    p.op("dve", ["GC0"], ["GC1"], lambda E: E.tensor_scalar(out=HCL, in0=CL, scalar1=0.5, scalar2=None, op0=ALU.mult))
    p.op("dve", ["PAR"], ["GC2"], lambda E: E.tensor_scalar(out=HBA, in0=par("ba"), scalar1=0.5, scalar2=None, op0=ALU.mult))
    p.op("dve", ["PAR"], ["GC3"], lambda E: E.tensor_scalar(out=HBX, in0=par("bx"), scalar1=0.5, scalar2=None, op0=ALU.mult))
    p.op("dve", ["GC0"], ["GC5"], lambda E: E.tensor_scalar(out=NHCL, in0=CL, scalar1=-0.5, scalar2=None, op0=ALU.mult))
    GCK = ["GC0", "GC1", "GC2", "GC3", "GC5"]

    def ada(L, j0, j1, done_key=None):
        b0, pk = alloc_ps(1)
        n = j1 - j0
        for j in range(j0, j1):
            wt, wk = load_w(wada_d[L, j], 1024, extra=(done_key if j == j1 - 1 else None))

            def emit(E, j=j, wt=wt):
                last = None
                for kc in range(8):
                    last = E.matmul(PS[:, b0 * 512 + 2 * (j - j0):b0 * 512 + 2 * (j - j0) + 2],
                                    lhsT=wt[:, kc * 128:(kc + 1) * 128], rhs=SBF[:, kc, :], start=(kc == 0), stop=(kc == 7))
                return last
            p.op("pe", wk + ["SBF"], pk, emit)
        psv = PS[:, b0 * 512:b0 * 512 + 2 * n].rearrange("q (j t) -> q t j", t=2)
        o, _ = _PAR["bada"]
        for t in range(2):
            p.op("dve", pk + ["PAR"], [f"MOD{L}_{j0}_{t}"], lambda E, t=t: E.tensor_tensor(
                out=MOD[:, L, t, j0:j1], in0=psv[:, t, :], in1=PAR[:, o + L * 48 + j0:o + L * 48 + j1], op=ALU.add))
        ks = [f"MOD{L}_{j0}_{t}" for t in range(2)]
        for j in range(j0, j1):
            modkey[(L, j)] = ks
        return ks

    modkey = {}

    def mks(L, j0, j1):
        out = []
        for j in range(j0, j1):
            for k in modkey[(L, j)]:
                if k not in out:
                    out.append(k)
        return out

    def run(gen):
        for _ in gen:
            pass

    def interleave(*gens):
        gens = list(gens)
        while gens:
            for g in list(gens):
                try:
                    next(g)
                except StopIteration:
                    gens.remove(g)

    def mod(L, t, j):
        return MOD[:, L, t, j:j + 1]

    def derive(L, which, keys):
        o, _ = _PAR["ng"]
        if which == 1:
            p.op("dve", keys + ["PAR"], [f"DER{L}_0"], lambda E: E.scalar_tensor_tensor(
                out=DER[:, L, 0, :], in0=MOD[:, L, 0, 8:16], scalar=1.0, in1=PAR[:, o + L * 16:o + L * 16 + 8],
                op0=ALU.add, op1=ALU.mult))
            p.op("dve", keys + ["PAR"], [f"DER{L}_2"], lambda E: E.scalar_tensor_tensor(
                out=DER[:, L, 2, :], in0=MOD[:, L, 1, 8:16], scalar=1.0, in1=PAR[:, o + L * 16:o + L * 16 + 8],
                op0=ALU.add, op1=ALU.mult))
        else:
            p.op("dve", keys + ["PAR"], [f"DER{L}_1"], lambda E: E.scalar_tensor_tensor(
                out=DER[:, L, 1, :], in0=MOD[:, L, 0, 32:40], scalar=1.0, in1=PAR[:, o + L * 16 + 8:o + L * 16 + 16],
                op0=ALU.add, op1=ALU.mult))

    def norm(T, src, src_keys, A, SH, dst, dst_keys, sc_keys, final=False):
        b0, pk = alloc_ps(nbanks(T))
        held = set(range(b0, b0 + nbanks(T)))
        reserved.update(held)
        psv = PS[:, b0 * 512:b0 * 512 + T]
        for c in range(NCH):
            sq = W16[:, c % 2, 0:T]
            sk = f"W16_{c % 2}"
            if c % 2 == 0 or T < 1024:
                p.op("act", [src_keys[c]], [sk], lambda E, c=c, sq=sq: E.activation(out=sq, in_=src(c), func=AF.Square))
            else:
                p.op("dve", [src_keys[c]], [sk], lambda E, c=c, sq=sq: E.tensor_tensor(out=sq, in0=src(c), in1=src(c),
                                                                                     op=ALU.mult))

            def emit(E, c=c, sq=sq):
                last = None
                for (a, b) in tgs(T):
                    last = E.matmul(psv[:, a:b], lhsT=ONES[:], rhs=sq[:, a:b], start=(c == 0), stop=(c == NCH - 1))
                return last
            p.op("pe", [sk, "ONES"], pk, emit)
            yield
        p.op("act", pk + ["EPSC"], pk, lambda E: E.activation(out=psv, in_=psv, func=AF.Ln, scale=1.0 / D, bias=EPSC[:, 0:1]))
        p.op("act", pk, pk, lambda E: E.activation(out=psv, in_=psv, func=AF.Exp, scale=-0.5))
        HT = min(T, 1024)
        i = 0
        dk = dst_keys if callable(dst_keys) else (lambda c, a: [dst_keys[c]])
        last = (NCH - 1, T - HT)
        halves = list(range(0, T, HT))
        order = [(c, a) for c in range(NCH) for a in halves] if final else [(c, a) for a in halves for c in range(NCH)]
        last = order[-1]
        for (c, a) in order:
            if True:
                if final:
                    p.op("dve", [src_keys[c]] + pk + sc_keys, dk(c, a), lambda E, c=c, a=a: E.scalar_tensor_tensor(
                        out=dst(c)[:, a:a + HT], in0=src(c)[:, a:a + HT], scalar=A(c), in1=psv[:, a:a + HT],
                        op0=ALU.mult, op1=ALU.mult))
                    if (c, a) == last:
                        reserved.difference_update(held)
                    yield
                    continue
                tmp = W32[:, i % 2, 0:HT]
                tk = f"W32_{i % 2}"
                i += 1
                p.op("dve", [src_keys[c]] + pk + sc_keys, [tk], lambda E, c=c, a=a, tmp=tmp: E.scalar_tensor_tensor(
                    out=tmp, in0=src(c)[:, a:a + HT], scalar=A(c), in1=psv[:, a:a + HT], op0=ALU.mult, op1=ALU.mult))
                p.op("act", [tk] + sc_keys, dk(c, a), lambda E, c=c, a=a, tmp=tmp: E.activation(
                    out=dst(c)[:, a:a + HT], in_=tmp, func=AF.Identity, bias=SH(c), scale=1.0))
                if (c, a) == last:
                    reserved.difference_update(held)
                yield

    def rec_branch(T, hsrc, hkeys, U, Ukeys):
        o_w, _ = _PAR["rcw"]
        o_b, _ = _PAR["rcb"]
        HT = min(T, 1024)
        i = 0
        for c in range(NR):
            wt, wk = load_w(win_d[10 + c], 1024)
            psv, pk = mm_job(T, wt, wk, list(range(8)), hsrc, hkeys)
            w = [PAR[:, o_w + c * 4 + k:o_w + c * 4 + k + 1] for k in range(4)]
            bb = PAR[:, o_b + c:o_b + c + 1]
            for t0 in range(0, T, HT):
                t1 = t0 + HT
                acc = W32[:, i % 2, 0:HT]
                ak = f"W32_{i % 2}"
                i += 1
                jlo = 1 if t0 == 0 else 0

                def emitA(E, acc=acc, t0=t0, t1=t1, jlo=jlo, w=w, bb=bb, psv=psv):
                    if jlo:
                        E.activation(out=acc[:, 0:1], in_=psv[:, 0:1], func=AF.Identity, bias=bb, scale=0.0)
                    return E.activation(out=acc[:, jlo:HT], in_=psv[:, t0 + jlo - 1:t1 - 1], func=AF.Identity, bias=bb,
                                        scale=w[0])
                p.op("act", pk + ["PAR"], [ak], emitA)
                for k, off in ((2, 1), (3, 2)):
                    n = min(t1 + off, T) - (t0 + off)
                    p.op("dve", pk + [ak, "PAR"], [ak], lambda E, acc=acc, t0=t0, off=off, n=n, k=k, w=w, psv=psv:
                         E.scalar_tensor_tensor(out=acc[:, 0:n], in0=psv[:, t0 + off:t0 + off + n], scalar=w[k],
                                                in1=acc[:, 0:n], op0=ALU.mult, op1=ALU.add))
                p.op("dve", pk + [ak, "PAR"], [Ukeys[c]], lambda E, acc=acc, t0=t0, t1=t1, c=c, w=w, psv=psv:
                     E.scalar_tensor_tensor(out=U(c)[:, t0:t1], in0=psv[:, t0:t1], scalar=w[1], in1=acc, op0=ALU.mult,
                                            op1=ALU.add))

    def gate_phase(T, U, Ukeys, is_ctx, G=None, Gkeys=None):
        NH = 2 if T > 1024 else 1
        HT = T // NH
        unit = [0]
        Y0 = HF[:, 6 * 1024:6 * 1024 + T]
        Y0k = ["H6", "H7"]
        for c in range(NR):
            ks = kset(c)
            y1_tiles = {}
            for d in (0, 1):
                pss = []
                for g in (0, 1):
                    wt, wk = load_w(wgate_d[(d * 2 + g) * 10 + c], 384)
                    psv, pk = mm_job(T, wt, wk, ks, U, [Ukeys[k] for k in ks])
                    pss.append((psv, pk))
                (psR, pkR), (psI, pkI) = pss
                cl = CL[:, d * NR + c:d * NR + c + 1]
                hcl = HCL[:, d * NR + c:d * NR + c + 1]
                nhcl = NHCL[:, d * NR + c:d * NR + c + 1]
                hba = HBA[:, d * NR + c:d * NR + c + 1]
                hbx = HBX[:, d * NR + c:d * NR + c + 1]
                order = list(range(NH)) if d == 0 else list(range(NH - 1, -1, -1))
                tiles = []
                for hh in order:
                    i = unit[0] % 3
                    unit[0] += 1
                    if i < 2:
                        At = HF[:, (0 + i) * 1024:(0 + i) * 1024 + HT]
                        St = HF[:, (2 + i) * 1024:(2 + i) * 1024 + HT]
                        It = HF[:, (4 + i) * 1024:(4 + i) * 1024 + HT]
                        Ak, Sk, Ik = f"H{i}", f"H{2 + i}", f"H{4 + i}"
                    else:
                        At, St, It = W16F[0][:, 0:HT], W16F[1][:, 0:HT], START[:, 0:HT]
                        Ak, Sk, Ik = "W16_0", "W16_1", "START"
                    a0 = hh * HT
                    p.op("act", pkR + GCK, [Ak], lambda E, At=At, a0=a0, hba=hba, psR=psR: E.activation(
                        out=At, in_=psR[:, a0:a0 + HT], func=AF.Tanh, bias=hba, scale=0.5))
                    p.op("act", pkI + GCK, [Ik], lambda E, It=It, a0=a0, hbx=hbx, psI=psI: E.activation(
                        out=It, in_=psI[:, a0:a0 + HT], func=AF.Tanh, bias=hbx, scale=0.5))
                    last_half = (hh == order[-1])
                    if last_half and not is_ctx:
                        yield
                    p.op("act", [Ak] + GCK, [Sk], lambda E, At=At, St=St, cl=cl: E.activation(
                        out=St, in_=At, func=AF.Exp, bias=cl, scale=cl))
                    p.op("act", [Ak] + GCK, [Ak], lambda E, At=At, hcl=hcl: E.activation(
                        out=At, in_=At, func=AF.Exp, bias=hcl, scale=hcl))
                    tiles.append((hh, At, St, It, Ak, Sk, Ik))
                for (hh, At, St, It, Ak, Sk, Ik) in tiles:
                    p.op("act", [Sk], [Sk], lambda E, St=St: E.activation(out=St, in_=St, func=AF.Sqrt, bias=1.0,
                                                                         scale=-1.0))
                prev = None
                for (hh, At, St, It, Ak, Sk, Ik) in tiles:
                    a0 = hh * HT
                    p.op("dve", [Ik, Sk], [Ik], lambda E, It=It, St=St: E.scalar_tensor_tensor(
                        out=It, in0=It, scalar=1.0, in1=St, op0=ALU.add, op1=ALU.mult))
                    p.op("dve", [Ik, Ukeys[c]], [Ik], lambda E, It=It, a0=a0, c=c: E.scalar_tensor_tensor(
                        out=It, in0=It, scalar=0.5, in1=U(c)[:, a0:a0 + HT], op0=ALU.mult, op1=ALU.mult))
                    if d == 0:
                        yt = Y0[:, a0:a0 + HT]
                        ykeys = Y0k
                    else:
                        j = hh % 2
                        yt = W32[:, j, 0:HT]
                        ykeys = [f"W32_{j}"]
                        y1_tiles[hh] = (yt, ykeys)
                    if prev is None:
                        init = 0.0 if is_ctx else H0[:, d, c:c + 1]
                        ikeys = [] if is_ctx else ["H0"]
                    else:
                        pyt, pykeys = prev
                        init = pyt[:, HT - 1:HT] if d == 0 else pyt[:, 0:1]
                        ikeys = pykeys
                    if d == 0:
                        p.op("dve", [Ak, Ik] + ikeys, ykeys, lambda E, yt=yt, At=At, It=It, init=init:
                             E.tensor_tensor_scan(out=yt, data0=At, data1=It, initial=init, op0=ALU.mult, op1=ALU.add))
                    else:
                        p.op("dve", [Ak, Ik] + ikeys, ykeys, lambda E, yt=yt, At=At, It=It, init=init:
                             E.tensor_tensor_scan(out=yt[:, ::-1], data0=At[:, ::-1], data1=It[:, ::-1], initial=init,
                                                  op0=ALU.mult, op1=ALU.add))
                    prev = (yt, ykeys)
                if is_ctx:
                    yt, ykeys = prev
                    col = yt[:, HT - 1:HT] if d == 0 else yt[:, 0:1]
                    p.op("dve", ykeys, ["H0"], lambda E, col=col, d=d, c=c: E.tensor_copy(out=H0[:, d, c:c + 1], in_=col))
            if not is_ctx:
                for hh in range(NH - 1, -1, -1):
                    yt, ykeys = y1_tiles[hh]
                    a0 = hh * HT
                    p.op("dve", ykeys + Y0k, ykeys, lambda E, yt=yt, a0=a0: E.tensor_tensor(
                        out=yt, in0=yt, in1=Y0[:, a0:a0 + HT], op=ALU.add))
                    p.op("dve", ykeys + [Gkeys[c]], [Gkeys[c]], lambda E, yt=yt, a0=a0, c=c: E.tensor_tensor(
                        out=G(c)[:, a0:a0 + HT], in0=yt, in1=G(c)[:, a0:a0 + HT], op=ALU.mult))

    def gate_phase_ctx(U, Ukeys):
        T = T_CTX
        units = [(c, d) for d in (0, 1) for c in range(NR)]
        cA = lambda u: GUF[:, 5120 + u * 256:5120 + (u + 1) * 256]
        cS = lambda u: GUF[:, 10240 + u * 256:10240 + (u + 1) * 256]
        cI = lambda u: GUF[:, 15360 + u * 256:15360 + (u + 1) * 256]
        cTh = [GUF[:, 4352:4608], GUF[:, 4608:4864]]
        cY = GUF[:, 4864:5120]
        allk = []
        WV = 2
        for w0 in range(0, len(units), WV):
            wave = []
            for u in range(w0, w0 + WV):
                c, d = units[u]
                ks = kset(c)
                pss = []
                for g in (0, 1):
                    wt, wk = load_w(wgate_d[(d * 2 + g) * 10 + c], 384)
                    psv, pk = mm_job(T, wt, wk, ks, U, [Ukeys[k] for k in ks])
                    pss.append((psv, pk))
                sc = [t_[:, d * NR + c:d * NR + c + 1] for t_ in (CL, HCL, HBA, HBX)]
                wave.append((u, pss, sc))
                allk.extend([f"cA{u}", f"cS{u}", f"cI{u}"])
            for (u, pss, sc) in wave:
                p.op("act", pss[0][1] + GCK, [f"cA{u}"], lambda E: E.activation(out=cA(u), in_=pss[0][0], func=AF.Tanh,
                                                                                  bias=sc[2], scale=0.5))
            for (u, pss, sc) in wave:
                p.op("act", pss[1][1] + GCK, [f"cI{u}"], lambda E: E.activation(out=cI(u), in_=pss[1][0], func=AF.Tanh,
                                                                                  bias=sc[3], scale=0.5))
            for (u, pss, sc) in wave:
                p.op("act", [f"cA{u}"] + GCK, [f"cS{u}"], lambda E: E.activation(out=cS(u), in_=cA(u), func=AF.Exp,
                                                                                   bias=sc[0], scale=sc[0]))
            for (u, pss, sc) in wave:
                p.op("act", [f"cA{u}"] + GCK, [f"cA{u}"], lambda E: E.activation(out=cA(u), in_=cA(u), func=AF.Exp,
                                                                                   bias=sc[1], scale=sc[1]))
            yield
        NU = len(units)
        cAall = GUF[:, 5120:5120 + NU * 256]
        cSall = GUF[:, 10240:10240 + NU * 256]
        cIall = GUF[:, 15360:15360 + NU * 256]
        Aks = [f"cA{u}" for u in range(NU)]
        Sks = [f"cS{u}" for u in range(NU)]
        Iks = [f"cI{u}" for u in range(NU)]
        p.op("act", Sks, Sks, lambda E: E.activation(out=cSall, in_=cSall, func=AF.Sqrt, bias=1.0, scale=-1.0))
        yield
        hw = NU // 2 * 256
        p.op("dve", Aks, Aks, lambda E: E.memset(cAall[:, 0:hw:256], 0.0))
        p.op("dve", Aks, Aks, lambda E: E.memset(cAall[:, hw + 255:2 * hw:256], 0.0))
        p.op("dve", Iks + Sks, Iks, lambda E: E.scalar_tensor_tensor(out=cIall, in0=cIall, scalar=1.0, in1=cSall,
                                                                       op0=ALU.add, op1=ALU.mult))
        ucall = GU2[:, 3 * 2048:3 * 2048 + NR * T_CTX]
        for d in range(2):
            p.op("dve", Iks + ["GU3", "GU4"], Iks, lambda E: E.scalar_tensor_tensor(
                out=cIall[:, d * hw:(d + 1) * hw], in0=cIall[:, d * hw:(d + 1) * hw], scalar=0.5, in1=ucall,
                op0=ALU.mult, op1=ALU.mult))
        yield
        p.op("dve", Aks + Iks + Sks, Sks, lambda E: E.tensor_tensor_scan(
            out=cSall[:, 0:hw], data0=cAall[:, 0:hw], data1=cIall[:, 0:hw], initial=0.0, op0=ALU.mult, op1=ALU.add))
        p.op("dve", Aks + Iks + Sks, Sks, lambda E: E.tensor_tensor_scan(
            out=cSall[:, hw:2 * hw][:, ::-1], data0=cAall[:, hw:2 * hw][:, ::-1], data1=cIall[:, hw:2 * hw][:, ::-1],
            initial=0.0, op0=ALU.mult, op1=ALU.add))
        p.op("dve", Sks, ["H0"], lambda E: E.tensor_copy(out=H0[:, 0, :], in_=cSall[:, 255:hw:256]))
        p.op("dve", Sks + ["H0"], ["H0"], lambda E: E.tensor_copy(out=H0[:, 1, :], in_=cSall[:, hw:2 * hw:256]))
        yield
        p.op("dve", allk + ["cY", "cTh0", "cTh1"], [f"GU{j}" for j in range(4, 20)], lambda E: E.memset(DUMMY[:], 0.0))

    def mlp(L, gkeys):
        i = 0
        for q in range(2):
            for m in range(16):
                wt, wk = load_w(mlpin_d[L, q * 16 + m], 1024)
                for a0 in (0, 1024):
                    psv, pk = mm_job(1024, wt, wk, list(range(8)), lambda k, a0=a0: H[:, k, a0:a0 + 1024], Hhalf(a0))
                    rt = W16[:, i % 2, 0:1024]
                    rk = f"W16_{i % 2}"
                    i += 1
                    p.op("act", pk, [rk], lambda E, rt=rt, psv=psv: E.activation(out=rt, in_=psv, func=AF.Relu))
                    p.op("dve", [rk], [f"GU{m}"], lambda E, rt=rt, m=m, a0=a0: E.tensor_tensor(
                        out=GU[:, m, a0:a0 + 1024], in0=rt, in1=rt, op=ALU.mult))
            for mo in range(NCH):
                wt, wk = load_w(mlpout_d[L, q * 8 + mo], 2048)
                psv, pk = mm_job(T_LAT, wt, wk, list(range(16)), lambda k: GU[:, k, :], [f"GU{k}" for k in range(16)])
                p.op("dve", pk + gkeys + [f"X{mo}"], [f"X{mo}"], lambda E, mo=mo, psv=psv: E.scalar_tensor_tensor(
                    out=X[:, mo, :], in0=psv, scalar=mod(L, 0, 40 + mo), in1=X[:, mo, :], op0=ALU.mult, op1=ALU.add))

    Xk = [f"X{c}" for c in range(NCH)]
    Hk = [f"H{c}" for c in range(NCH)] + [f"HB{c}" for c in range(NCH)]
    hkey = lambda c, a: [f"H{c}"] if a == 0 else [f"HB{c}"]
    Hhalf = lambda a: [hkey(k, a)[0] for k in range(NCH)]

    def h_fence():
        p.op("dve", [], Hk, lambda E: E.memset(DUMMY[:], 0.0))

    L = 0
    mk_a = ["MOD0_0_0", "MOD0_0_1"]
    g_cn = norm(T_CTX, lambda c: XC[:, c, :], ["GU0"] * 4 + ["GU1"] * 4, lambda c: DER[:, 0, 2, c:c + 1],
                lambda c: mod(0, 1, c), lambda c: HC[:, c, :], ["GU2"] * 8, mk_a + ["DER0_2"])
    for _ in range(NCH):
        next(g_cn)
    ada(0, 0, 16, done_key="ADA0_DONE")
    assert mks(0, 0, 16) == mk_a
    for c in range(NCH):
        p.dma("sp", f"xin{c}", [(X[:, c, :], xT_v[:, c, :])], ["ADA0_DONE"], [f"X{c}"])
    derive(0, 1, mk_a)
    run(g_cn)
    UCk = ["GU3"] * 8 + ["GU4"] * 2
    rec_branch(T_CTX, lambda k: HC[:, k, :], ["GU2"], lambda c: UC[:, c, :], UCk)
    pos_embed()
    g_cg = gate_phase_ctx(lambda c: UC[:, c, :], UCk)
    g_ln = norm(T_LAT, lambda c: X[:, c, :], Xk, lambda c: DER[:, 0, 0, c:c + 1], lambda c: mod(0, 0, c),
                lambda c: H[:, c, :], hkey, mk_a + ["DER0_0"])
    for _ in range(3):
        next(g_cg)
    for _ in g_cg:
        for _ in range(3):
            next(g_ln, None)
    run(g_ln)
    Uk = [f"GU{10 + c}" for c in range(NR)]
    Gk = [f"GU{c}" for c in range(NR)]
    rec_branch(T_LAT, lambda k: H[:, k, :], Hk, lambda c: GU[:, 10 + c, :], Uk)
    for c in range(NR):
        wt, wk = load_w(win_d[c], 1024)
        psv, pk = mm_job(T_LAT, wt, wk, list(range(8)), lambda k: H[:, k, :], Hk)
        p.op("act", pk, [Gk[c]], lambda E, c=c, psv=psv: E.activation(out=GU[:, c, :], in_=psv, func=AF.Gelu_apprx_tanh))
    groups = [(0, j, j + 4) for j in range(16, 48, 4)]
    if n_layers > 1:
        groups += [(1, j, j + 4) for j in range(0, 48, 4)]
    gi_ = iter(groups)
    h_fence()
    for _ in gate_phase(T_LAT, lambda c: GU[:, 10 + c, :], Uk, False, lambda c: GU[:, c, :], Gk):
        g_ = next(gi_, None)
        if g_ is not None:
            ada(*g_)
    for g_ in gi_:
        ada(*g_)
    h_fence()
    mk_b = mks(0, 16, 48)
    derive(0, 2, mk_b)
    if n_layers > 1:
        mk1_a = mks(1, 0, 24)
        mk1_b = mks(1, 24, 48)
    for mo in range(NCH):
        wt, wk = load_w(wout_d[mo], 1280)
        psv, pk = mm_job(T_LAT, wt, wk, list(range(NR)), lambda k: GU[:, k, :], Gk)
        p.op("dve", pk + mk_b + [f"X{mo}"], [f"X{mo}"], lambda E, mo=mo, psv=psv: E.scalar_tensor_tensor(
            out=X[:, mo, :], in0=psv, scalar=mod(0, 0, 16 + mo), in1=X[:, mo, :], op0=ALU.mult, op1=ALU.add))
    run(norm(T_LAT, lambda c: X[:, c, :], Xk, lambda c: DER[:, 0, 1, c:c + 1], lambda c: mod(0, 0, 24 + c),
             lambda c: H[:, c, :], hkey, mk_b + ["DER0_1"]))
    mlp(0, mk_b)

    if n_layers > 1:
        L = 1
        derive(1, 1, mk1_a)
        derive(1, 2, mk1_b)
        o_b2, _ = _PAR["bpw2"]
        p.op("dve", mk1_a + ["PAR"], ["DER1_3"], lambda E: E.tensor_tensor(
            out=DER[:, 1, 3, :], in0=MOD[:, 1, 0, 16:24], in1=PAR[:, o_b2:o_b2 + 8], op=ALU.mult))
        run(norm(T_LAT, lambda c: X[:, c, :], Xk, lambda c: DER[:, 1, 0, c:c + 1], lambda c: mod(1, 0, c),
                 lambda c: H[:, c, :], hkey, mk1_a + ["DER1_0"]))
        for i in range(2):
            def emit(E, i=i):
                E.memset(W16[:, i, 0:15], 0.0)
                return E.memset(W16[:, i, 15 + T_LAT:2080], 0.0)
            p.op("dve", [], [f"W16_{i}", f"Z{i}a", f"Z{i}b"], emit)
        DGB = [GU2[:, (16 + 2 * b) * 2048:(16 + 2 * b) * 2048 + KW * 128].rearrange("q (k m) -> q k m", k=KW) for b in range(2)]
        o_cw, _ = _PAR["ccw"]
        o_cb, _ = _PAR["ccb"]
        o_b1, _ = _PAR["bpw1"]
        ZC = GUF[:, 0:16 * 1024].rearrange("q (c t) -> q c t", c=NCH)
        gi = 0
        for c in range(NCH):
            db = c % 2
            dkeys = [f"GU{16 + 2 * db}", f"GU{17 + 2 * db}"]

            def emit(E, c=c, db=db):
                last = None
                for k in range(NPE):
                    last = E.tensor_scalar(out=DGB[db][:, k, :], in0=IDN[:], scalar1=PAR[:, o_cw + c * KW + k:o_cw + c * KW + k + 1],
                                           scalar2=None, op0=ALU.mult)
                return last
            p.op("dve", ["IDN", "PAR"], dkeys, emit)
            wtA, wkA = load_w(pw1_d[2 * c], 1024)
            wtB, wkB = load_w(pw1_d[2 * c + 1], 1024)
            zt = W16[:, c % 2, :]
            zka, zkb = f"Z{c % 2}a", f"Z{c % 2}b"
            for a0 in (0, 1024):
                psA, pkA = mm_job(1024, wtA, wkA, list(range(8)), lambda k, a0=a0: H[:, k, a0:a0 + 1024], Hhalf(a0))
                psB, pkB = mm_job(1024, wtB, wkB, list(range(8)), lambda k, a0=a0: H[:, k, a0:a0 + 1024], Hhalf(a0))
                sg = W32[:, gi % 2, :]
                sgk = f"W32_{gi % 2}"
                gi += 1
                p.op("act", pkB + ["PAR"], [sgk], lambda E, sg=sg, c=c, psB=psB: E.activation(
                    out=sg, in_=psB, func=AF.Sigmoid, bias=PAR[:, o_b1 + 8 + c:o_b1 + 9 + c], scale=1.0))
                p.op("dve", pkA + [sgk, "PAR"], [zka if a0 == 0 else zkb],
                     lambda E, sg=sg, a0=a0, c=c, psA=psA, zt=zt: E.scalar_tensor_tensor(
                         out=zt[:, 15 + a0:15 + a0 + 1024], in0=psA, scalar=PAR[:, o_b1 + c:o_b1 + c + 1],
                         in1=sg, op0=ALU.add, op1=ALU.mult))
            zkeys_tg = [[zka], [zka, zkb], [zka, zkb], [zkb]]
            for tg in range(4):
                b0, pk = alloc_ps(1)
                pv = PS[:, b0 * 512:(b0 + 1) * 512]
                bd, pkd = alloc_ps(1)
                pd = PS[:, bd * 512:(bd + 1) * 512]
                zck = f"GU{2 * c + tg // 2}"
                zcv = ZC[:, c, tg * 512:(tg + 1) * 512]

                def emit(E, tg=tg, db=db, pv=pv, zt=zt):
                    last = None
                    for k in range(NPE):
                        last = E.matmul(pv, lhsT=DGB[db][:, k, :], rhs=zt[:, tg * 512 + k:tg * 512 + k + 512],
                                        start=(k == 0), stop=(k == NPE - 1))
                    return last
                p.op("pe", dkeys + zkeys_tg[tg], pk, emit)

                tapsA = list(range(NPE, KW, 2))
                tapsB = list(range(NPE + 1, KW, 2))
                for i in range(max(len(tapsA), len(tapsB))):
                    if i < len(tapsA):
                        k = tapsA[i]
                        wk_ = PAR[:, o_cw + c * KW + k:o_cw + c * KW + k + 1]
                        src = zt[:, tg * 512 + k:tg * 512 + k + 512]
                        if i == 0:
                            p.op("dve", zkeys_tg[tg] + ["PAR"], pkd, lambda E: E.tensor_scalar(
                                out=pd, in0=src, scalar1=wk_, scalar2=PAR[:, o_cb + c:o_cb + c + 1], op0=ALU.mult, op1=ALU.add))
                        else:
                            p.op("dve", zkeys_tg[tg] + ["PAR"] + pkd, pkd, lambda E: E.scalar_tensor_tensor(
                                out=pd, in0=src, scalar=wk_, in1=pd, op0=ALU.mult, op1=ALU.add))
                    if i < len(tapsB):
                        k = tapsB[i]
                        wk_ = PAR[:, o_cw + c * KW + k:o_cw + c * KW + k + 1]
                        src = zt[:, tg * 512 + k:tg * 512 + k + 512]
                        if i == 0:
                            p.op("dve", zkeys_tg[tg] + ["PAR"], [zck], lambda E: E.tensor_scalar(
                                out=zcv, in0=src, scalar1=wk_, scalar2=None, op0=ALU.mult))
                        else:
                            p.op("dve", zkeys_tg[tg] + ["PAR", zck], [zck], lambda E: E.scalar_tensor_tensor(
                                out=zcv, in0=src, scalar=wk_, in1=zcv, op0=ALU.mult, op1=ALU.add))
                p.op("dve", pkd + [zck], [zck], lambda E: E.tensor_tensor(out=zcv, in0=zcv, in1=pd, op=ALU.add))
                p.op("dve", pk + [zck], [zck], lambda E, pv=pv, zcv=zcv: E.tensor_tensor(out=zcv, in0=zcv, in1=pv, op=ALU.add))
        p.op("dve", [], ["W16_0", "W16_1", "Z0a", "Z0b", "Z1a", "Z1b"], lambda E: E.memset(DUMMY[:], 0.0))
        o_g, _ = _PAR["lng"]
        o_lb, _ = _PAR["lnb"]
        st = []
        for hf in range(2):
            a0 = hf * 1024
            b1, pk1 = alloc_ps(2)
            b2, pk2 = alloc_ps(2)
            S1 = PS[:, b1 * 512:b1 * 512 + 1024]
            S2 = PS[:, b2 * 512:b2 * 512 + 1024]
            st.append((a0, pk1, pk2, S1, S2))
            for c in range(NCH):
                zk = f"GU{2 * c + hf}"
                tb = W16[:, c % 2, 0:1024]
                tq = W16[:, c % 2, 1024:2048]
                tk = f"W16_{c % 2}"
                p.op("act", [zk], [tk], lambda E, c=c, tb=tb: E.activation(out=tb, in_=ZC[:, c, a0:a0 + 1024],
                                                                               func=AF.Identity))
                p.op("act", [zk, tk], [tk], lambda E, c=c, tq=tq: E.activation(out=tq, in_=ZC[:, c, a0:a0 + 1024],
                                                                               func=AF.Square))

                def emit(E, c=c, tb=tb, tq=tq):
                    last = None
                    for t in range(2):
                        E.matmul(S1[:, t * 512:(t + 1) * 512], lhsT=ONES[:], rhs=tb[:, t * 512:(t + 1) * 512],
                                 start=(c == 0), stop=(c == NCH - 1))
                        last = E.matmul(S2[:, t * 512:(t + 1) * 512], lhsT=ONES[:], rhs=tq[:, t * 512:(t + 1) * 512],
                                        start=(c == 0), stop=(c == NCH - 1))
                    return last
                p.op("pe", [tk, "ONES"], pk1 + pk2, emit)
        for hf in range(2):
            a0, pk1, pk2, S1, S2 = st[hf]
            mean = W32[:, 0, :]
            var = W32[:, 1, :]
            p.op("act", pk1, ["W32_0"], lambda E: E.activation(out=mean, in_=S1, func=AF.Identity, scale=1.0 / D))
            p.op("dve", ["W32_0"], ["W32_1"], lambda E: E.tensor_tensor(out=var, in0=mean, in1=mean, op=ALU.mult))
            p.op("dve", pk2 + ["W32_1"], ["W32_1"], lambda E: E.scalar_tensor_tensor(
                out=var, in0=S2, scalar=1.0 / D, in1=var, op0=ALU.mult, op1=ALU.subtract))
            p.op("act", ["W32_1", "EPSC"], ["W32_1"], lambda E: E.activation(out=var, in_=var, func=AF.Ln, bias=EPSC[:, 0:1], scale=1.0))
            p.op("act", ["W32_1"], pk2, lambda E: E.activation(out=S2, in_=var, func=AF.Exp, scale=-0.5))
            p.op("dve", ["W32_0"] + pk2, pk1, lambda E: E.scalar_tensor_tensor(
                out=S1, in0=mean, scalar=-1.0, in1=S2, op0=ALU.mult, op1=ALU.mult))
        for hf in range(2):
            a0, pk1, pk2, S1, S2 = st[hf]
            for c in range(NCH):
                zk = f"GU{2 * c + hf}"
                zz = ZC[:, c, a0:a0 + 1024]
                p.op("dve", [zk] + pk2, [zk], lambda E, zz=zz: E.tensor_tensor(out=zz, in0=zz, in1=S2, op=ALU.mult))
                p.op("dve", [zk] + pk1, [zk], lambda E, zz=zz: E.tensor_tensor(out=zz, in0=zz, in1=S1, op=ALU.add))
                p.op("act", [zk, "PAR"], hkey(c, a0), lambda E, zz=zz, c=c: E.activation(
                    out=H[:, c, a0:a0 + 1024], in_=zz, func=AF.Silu, bias=PAR[:, o_lb + c:o_lb + c + 1],
                    scale=PAR[:, o_g + c:o_g + c + 1]))
        for mo in range(NCH):
            p.op("act", ["DER1_3", f"X{mo}"], [f"X{mo}"], lambda E, mo=mo: E.activation(
                out=X[:, mo, :], in_=X[:, mo, :], func=AF.Identity, bias=DER[:, 1, 3, mo:mo + 1], scale=1.0))
        for a0 in (0, 1024):
            for mo in range(NCH):
                wt, wk = load_w(pw2_d[mo], 1024)
                psv, pk = mm_job(1024, wt, wk, list(range(8)), lambda k, a0=a0: H[:, k, a0:a0 + 1024], Hhalf(a0))
                p.op("dve", pk + mk1_a + [f"X{mo}"], [f"X{mo}"], lambda E, mo=mo, psv=psv, a0=a0: E.scalar_tensor_tensor(
                    out=X[:, mo, a0:a0 + 1024], in0=psv, scalar=mod(1, 0, 16 + mo), in1=X[:, mo, a0:a0 + 1024],
                    op0=ALU.mult, op1=ALU.add))
        run(norm(T_LAT, lambda c: X[:, c, :], Xk, lambda c: DER[:, 1, 1, c:c + 1], lambda c: mod(1, 0, 24 + c),
                 lambda c: H[:, c, :], hkey, mk1_b + ["DER1_1"]))
        mlp(1, mk1_b)

    o_fg, _ = _PAR["fg"]
    run(norm(T_LAT, lambda c: X[:, c, :], Xk, lambda c: PAR[:, o_fg + c:o_fg + c + 1], None,
             lambda c: X[:, c, :], Xk, ["PAR"], final=True))
    outT_v = outT.rearrange("(c q) t -> q c t", q=128)
    toks = [p.dma("sp", f"out{c}", [(outT_v[:, c, :], X[:, c, :])], [Xk[c]], []) for c in range(NCH)]
    for tok in toks:
        nc.sync.wait_ge(tok[0], tok[1])
    return nc, es


def _fm(v, nchunk):
    return np.ascontiguousarray(np.asarray(v, np.float32).reshape(nchunk, 128).T)


def _tile(W, rows, cols):
    sub = W[rows][:, cols]
    kc = sub.shape[0] // 128
    return np.ascontiguousarray(sub.reshape(kc, 128, sub.shape[1]).transpose(1, 0, 2).reshape(128, kc * sub.shape[1]))


_NC_CACHE = {}


def _get_nc(n_layers):
    if n_layers not in _NC_CACHE:
        _NC_CACHE[n_layers] = build(n_layers)
    return _NC_CACHE[n_layers][0]


def prep_inputs(x, c, ctx, c_ctx, w_ada, b_ada, norm_g, rec_w_in, rec_conv_w, rec_conv_b, rec_lambda,
                rec_w_a, rec_b_a, rec_w_x, rec_b_x, rec_w_out, conf_w_pw1, conf_b_pw1, conf_conv_w, conf_conv_b,
                conf_ln_g, conf_ln_b, conf_w_pw2, conf_b_pw2, mlp_w_in, mlp_w_out, final_g):
    f = lambda a: np.asarray(a, np.float32)
    x, c, ctx, c_ctx = f(x), f(c), f(ctx), f(c_ctx)
    w_ada, mlp_w_in, mlp_w_out = f(w_ada), f(mlp_w_in), f(mlp_w_out)
    sl = lambda i: slice(i * 128, (i + 1) * 128)
    allk = lambda K: slice(0, K)
    wada = np.stack([np.stack([_tile(w_ada[L], allk(D), sl(j)) for j in range(48)]) for L in range(2)])
    win = np.stack([_tile(f(rec_w_in)[0], allk(D), sl(m)) for m in range(20)])
    wgate = np.zeros((40, 128, 384), np.float32)
    for d in range(2):
        for g, w in enumerate((f(rec_w_a)[0, d], f(rec_w_x)[0, d])):
            Wbd = np.zeros((R, R), np.float32)
            for h in range(16):
                Wbd[h * 80:(h + 1) * 80, h * 80:(h + 1) * 80] = w[h]
            for cc in range(NR):
                ks = kset(cc)
                for idx, k in enumerate(ks):
                    wgate[(d * 2 + g) * 10 + cc, :, idx * 128:(idx + 1) * 128] = Wbd[sl(k), sl(cc)]
    wout = np.stack([_tile(f(rec_w_out)[0], allk(R), sl(m)) for m in range(8)])
    mlpin = np.stack([np.stack([_tile(mlp_w_in[L], allk(D), sl(m)) for m in range(32)]) for L in range(2)])
    mlpout = np.stack([np.stack([_tile(mlp_w_out[L], slice(q * 2048, (q + 1) * 2048), sl(mo))
                                 for q in range(2) for mo in range(8)]) for L in range(2)])
    pw1 = np.stack([_tile(f(conf_w_pw1)[0], allk(D), sl(m)) for c8 in range(8) for m in (c8, 8 + c8)])
    pw2 = np.stack([_tile(f(conf_w_pw2)[0], allk(D), sl(m)) for m in range(8)])

    shared = dict(wada=wada, win=win, wgate=wgate, wout=wout, mlpin=mlpin, mlpout=mlpout, pw1=pw1, pw2=pw2)
    pbase = np.zeros((128, NPAR), np.float32)

    def put(name, arr):
        o, n = _PAR[name]
        pbase[:, o:o + n] = np.asarray(arr, np.float32).reshape(128, n)
    put("bada", np.stack([_fm(f(b_ada)[L], 48) for L in range(2)], axis=1))
    put("ng", np.stack([_fm(f(norm_g)[L, n], 8) for L in range(2) for n in range(2)], axis=1))
    put("fg", _fm(final_g, 8))
    put("rcw", np.stack([_fm(f(rec_conv_w)[0, k], NR) for k in range(4)], axis=2))
    put("rcb", _fm(f(rec_conv_b)[0], NR))
    put("lam", np.stack([_fm(f(rec_lambda)[0, d], NR) for d in range(2)], axis=1))
    put("ba", np.stack([_fm(f(rec_b_a)[0, d].reshape(-1), NR) for d in range(2)], axis=1))
    put("bx", np.stack([_fm(f(rec_b_x)[0, d].reshape(-1), NR) for d in range(2)], axis=1))
    put("bpw1", _fm(f(conf_b_pw1)[0], 16))
    put("ccw", np.stack([_fm(f(conf_conv_w)[0, k], 8) for k in range(KW)], axis=2))
    put("ccb", _fm(f(conf_conv_b)[0], 8))
    put("lng", _fm(f(conf_ln_g)[0], 8))
    put("lnb", _fm(f(conf_ln_b)[0], 8))
    put("bpw2", _fm(f(conf_b_pw2)[0], 8))
    in_maps = []
    for b in range(8):
        pb = pbase.copy()
        o, n = _PAR["cvec"]
        pb[:, o:o + n] = np.stack([_fm(c[b], 8), _fm(c_ctx, 8)], axis=2).reshape(128, 16)
        m = dict(shared)
        m["xT"] = np.ascontiguousarray(x[b].T)
        m["ctxT"] = np.ascontiguousarray(ctx[b].T)
        m["par"] = pb
        in_maps.append(m)
    return in_maps


def run(inputs, n_layers=2, trace=False):
    nc = _get_nc(n_layers)
    in_maps = prep_inputs(**inputs)
    res = run_bass_kernel_spmd(nc, in_maps, core_ids=list(range(8)), trace=trace)
    out = np.stack([np.ascontiguousarray(r["outT"].T) for r in res.results]).astype(np.float32)
    return out, res


def kernel(**inputs):
    out, _ = run(inputs, 2)
    return out
```

```python
import math
from contextlib import ExitStack

import numpy as np
import concourse.bass as bass
import concourse.mybir as mybir
from concourse.ap import AP
from concourse.bass_utils import run_bass_kernel_spmd

F32 = mybir.dt.float32
BF16 = mybir.dt.bfloat16
I32 = mybir.dt.int32
AF = mybir.ActivationFunctionType
ALU = mybir.AluOpType

D = 1024
T_LAT = 2048
T_CTX = 256
R = 1280
NCH = 8
NR = 10
DFF = 4096
KW = 31
EPS = 1e-6
RG_C = 8.0
NSLOT = 4
NPE = 23
EPOCH = 3000


def kset(c):
    b0 = (128 * c) // 80
    b1 = (128 * c + 127) // 80
    k0 = (80 * b0) // 128
    k1 = (80 * (b1 + 1) - 1) // 128
    return list(range(k0, k1 + 1))


_PAR = {}
_off = 0
for _name, _n in [("cvec", 16), ("bada", 96), ("ng", 32), ("fg", 8), ("rcw", 40), ("rcb", 10),
                  ("lam", 20), ("ba", 20), ("bx", 20), ("bpw1", 16), ("ccw", 8 * KW), ("ccb", 8),
                  ("lng", 8), ("lnb", 8), ("bpw2", 8)]:
    _PAR[_name] = (_off, _n)
    _off += _n
NPAR = _off


class Prog:
    def __init__(self, nc, es):
        self.nc = nc
        self.es = es
        self.E = {"pe": nc.tensor, "act": nc.scalar, "dve": nc.vector, "pool": nc.gpsimd, "sp": nc.sync}
        self.esem = {}
        self.ecnt = {e: 0 for e in self.E}
        self.waited = {e: {} for e in self.E}
        self.res = {}
        self.dsem = {}
        self.nsem = 0

    def _newsem(self, name):
        self.nsem += 1
        return self.es.enter_context(self.nc.semaphore(name))

    def _wait(self, eng, tok):
        sem, val, owner, sid = tok
        if owner == "pe" and eng == "pe":
            return
        if self.waited[eng].get(sid, 0) >= val:
            return
        self.E[eng].wait_ge(sem, val)
        self.waited[eng][sid] = val

    def _deps(self, eng, reads, writes):
        toks = []
        for k in reads:
            r = self.res.get(k)
            if r and r[0]:
                toks.append(r[0])
        for k in writes:
            r = self.res.get(k)
            if r:
                if r[0]:
                    toks.append(r[0])
                toks.extend(r[1].values())
        for t in toks:
            self._wait(eng, t)

    def _commit(self, tok, reads, writes):
        for k in reads:
            self.res.setdefault(k, [None, {}])[1][tok[3]] = tok
        for k in writes:
            self.res[k] = [tok, {}]

    def op(self, eng, reads, writes, emit):
        self._deps(eng, reads, writes)
        inst = emit(self.E[eng])
        n = self.ecnt[eng]
        self.ecnt[eng] += 1
        ep = n // EPOCH
        if (eng, ep) not in self.esem:
            self.esem[(eng, ep)] = self._newsem(f"s_{eng}_{ep}")
        sem = self.esem[(eng, ep)]
        inst.then_inc(sem, 1)
        tok = (sem, n - ep * EPOCH + 1, eng, f"{eng}_{ep}")
        self._commit(tok, reads, writes)
        return tok

    def dma(self, queue, slot, pairs, reads, writes):
        self._deps(queue, reads, writes)
        if slot not in self.dsem:
            self.dsem[slot] = [self._newsem(f"d_{slot}"), 0]
        ent = self.dsem[slot]
        for (o, i) in pairs:
            self.E[queue].dma_start(out=o, in_=i).then_inc(ent[0], 16)
            ent[1] += 16
        tok = (ent[0], ent[1], "dma", f"d_{slot}")
        self._commit(tok, reads, writes)
        return tok


def build(n_layers=2):
    es = ExitStack()
    nc = bass.Bass("TRN2", target_bir_lowering=False, dynamic_dma_scratch_size=4096)
    p = Prog(nc, es)

    def dram(name, shape, kind="ExternalInput", dt=F32):
        return nc.dram_tensor(name, shape, dt, kind=kind).ap()

    xT = dram("xT", [D, T_LAT])
    ctxT = dram("ctxT", [D, T_CTX])
    par_d = dram("par", [128, NPAR])
    wada_d = dram("wada", [2, 48, 128, 1024])
    win_d = dram("win", [20, 128, 1024])
    wgate_d = dram("wgate", [40, 128, 384])
    wout_d = dram("wout", [8, 128, 1280])
    mlpin_d = dram("mlpin", [2, 32, 128, 1024])
    mlpout_d = dram("mlpout", [2, 16, 128, 2048])
    pw1_d = dram("pw1", [16, 128, 1024])
    pw2_d = dram("pw2", [8, 128, 1024])
    outT = dram("outT", [D, T_LAT], kind="ExternalOutput")

    def sb(name, shape, dt):
        return es.enter_context(nc.sbuf_tensor(name, shape, dt))

    X = sb("X", [128, NCH, T_LAT], F32)
    H2 = sb("H", [128, NCH * T_LAT], BF16)
    GU2 = sb("GU", [128, 20 * T_LAT], BF16)
    H = H2[:].rearrange("q (a t) -> q a t", a=NCH)
    GU = GU2[:].rearrange("q (a t) -> q a t", a=20)
    RING = sb("RING", [128, NSLOT * 2048], BF16)
    W16 = sb("W16", [128, 2, 2080], BF16)
    W32 = sb("W32", [128, 2, 1024], F32)
    PAR = sb("PAR", [128, NPAR], F32)
    MOD = sb("MOD", [128, 2, 2, 48], F32)
    DER = sb("DER", [128, 2, 5, 8], F32)
    SBF = sb("SBF", [128, 8, 2], BF16)
    ONES = sb("ONES", [128, 128], BF16)
    IDN = sb("IDN", [128, 128], BF16)
    GC = sb("GC", [128, 6, 20], F32)
    H0 = sb("H0", [128, 2, NR], F32)
    START = sb("START", [128, 1728], F32)
    IDXF = START[:, 0:64]
    QVF = START[:, 64:66]
    OM = START[:, 66:68]
    QT = START[:, 128:640]
    KF = START[:, 640:1152]
    KI = START[:, 640:1152].bitcast(I32)
    PE_ = START[:, 1152:1664].rearrange("q (a b) -> q a b", a=8)
    PS = es.enter_context(nc.psum_tensor("PS", [128, 8 * 512], F32))

    HF = H2[:].bitcast(F32)
    GUF = GU2[:].bitcast(F32)
    W16F = [W16[:, i, :].bitcast(F32) for i in range(2)]

    def par(name, a=0, b=None):
        o, n = _PAR[name]
        b = n if b is None else b
        return PAR[:, o + a:o + b]

    psp = [0]
    reserved = set()

    def alloc_ps(nb):
        s = psp[0]
        for _ in range(32):
            s = ((s + nb - 1) // nb) * nb
            if s + nb > 8:
                s = 0
            if not any(b in reserved for b in range(s, s + nb)):
                break
            s += nb
        else:
            raise RuntimeError("no free PSUM banks")
        psp[0] = s + nb
        return s, [f"P{b}" for b in range(s, s + nb)]

    def nbanks(T):
        return max(1, T // 512)

    wcount = [0]

    NSUB = NSLOT * 2
    wptr = [0]

    def load_w(src_ap, ncols, extra=None):
        need = 1 if ncols <= 1024 else 2
        s0 = wptr[0]
        if need == 2 and s0 % 2:
            s0 += 1
        if s0 + need > NSUB:
            s0 = 0
        wptr[0] = (s0 + need) % NSUB
        keys = [f"R{s0 + i}" for i in range(need)]
        p.dma("pool", f"R{s0}", [(RING[:, s0 * 1024:s0 * 1024 + ncols], src_ap)], [], keys + ([extra] if extra else []))
        return RING[:, s0 * 1024:(s0 + need) * 1024], keys

    def tgs(T):
        return [(a, min(T, a + 512)) for a in range(0, T, 512)]

    def mm_job(T, wt, wkey, kcs, rhs_fn, rhs_keys):
        b0, pk = alloc_ps(nbanks(T))
        psv = PS[:, b0 * 512:b0 * 512 + T]

        def emit(E):
            last = None
            for (a, b) in tgs(T):
                for idx, k in enumerate(kcs):
                    last = E.matmul(psv[:, a:b], lhsT=wt[:, idx * 128:(idx + 1) * 128], rhs=rhs_fn(k)[:, a:b],
                                    start=(idx == 0), stop=(idx == len(kcs) - 1))
            return last
        p.op("pe", wkey + rhs_keys, pk, emit)
        return psv, pk

    p.dma("sp", "par", [(PAR[:], par_d)], [], ["PAR"])
    xT_v = xT.rearrange("(c q) t -> q c t", q=128)
    XC = GUF[:, 0:2048].rearrange("q (c t) -> q c t", c=NCH)
    HC = GU2[:, 2 * 2048:3 * 2048].rearrange("q (c t) -> q c t", c=NCH)
    UC = GU2[:, 3 * 2048:3 * 2048 + NR * T_CTX].rearrange("q (c t) -> q c t", c=NR)
    ctxT_v = ctxT.rearrange("(c q) t -> q c t", q=128)
    p.dma("sp", "cin", [(XC, ctxT_v)], [], ["GU0", "GU1"])
    DUMMY = sb("DUMMY", [128, 2], F32)
    EPSC = sb("EPSC", [128, 2], F32)
    p.op("dve", [], ["EPSC"], lambda E: E.memset(EPSC[:], EPS))

    p.op("dve", [], ["ONES"], lambda E: E.memset(ONES[:], 1.0))
    p.op("pool", [], ["ST"], lambda E: E.iota(KI[:, 0:128], pattern=[[1, 128]], base=0, channel_multiplier=-1))
    p.op("dve", ["ST"], ["IDN"], lambda E: E.tensor_scalar(out=IDN[:], in0=KI[:, 0:128], scalar1=0.0, scalar2=None,
                                                           op0=ALU.is_equal))
    p.op("pool", ["IDN"], ["ST"], lambda E: E.iota(KI[:, 128:192], pattern=[[1, 64]], base=0, channel_multiplier=0))
    p.op("pool", ["ST"], ["ST"], lambda E: E.iota(KI[:, 192:194], pattern=[[128, 2]], base=0, channel_multiplier=1))
    p.op("dve", ["ST"], ["ST"], lambda E: E.tensor_copy(out=START[:, 0:66], in_=KI[:, 128:194]))
    p.op("act", ["ST"], ["ST"], lambda E: E.activation(out=OM, in_=QVF, func=AF.Exp, scale=-math.log(10000.0) / 256.0))
    p.op("dve", ["ST"], ["ST"], lambda E: E.tensor_scalar(out=OM, in0=OM, scalar1=1.0 / (2 * math.pi), scalar2=None,
                                                          op0=ALU.mult))
    for c in range(NCH):
        e = c % 2
        phase = 0.25 if (c // 2) % 2 == 1 else 0.0
        p.op("dve", ["ST"], ["ST"], lambda E: E.tensor_scalar(out=QT[:, c * 64:(c + 1) * 64], in0=IDXF, scalar1=OM[:, e:e + 1],
                                                              scalar2=phase, op0=ALU.mult, op1=ALU.add))
    p.op("dve", ["ST"], ["ST"], lambda E: E.tensor_copy(out=KI, in_=QT))
    p.op("dve", ["ST"], ["ST"], lambda E: E.tensor_copy(out=KF, in_=KI))
    p.op("dve", ["ST"], ["ST"], lambda E: E.tensor_tensor(out=QT, in0=QT, in1=KF, op=ALU.subtract))
    p.op("dve", ["ST"], ["ST"], lambda E: E.tensor_scalar(out=KF, in0=QT, scalar1=0.5, scalar2=None, op0=ALU.is_gt))
    p.op("dve", ["ST"], ["ST"], lambda E: E.tensor_tensor(out=QT, in0=QT, in1=KF, op=ALU.subtract))
    p.op("dve", ["ST"], ["ST"], lambda E: E.tensor_scalar(out=KF, in0=QT, scalar1=-0.5, scalar2=None, op0=ALU.is_lt))
    p.op("dve", ["ST"], ["ST"], lambda E: E.tensor_tensor(out=QT, in0=QT, in1=KF, op=ALU.add))
    p.op("act", ["ST"], ["ST"], lambda E: E.activation(out=START[:, 1152:1664], in_=QT, func=AF.Sin, scale=2 * math.pi))

    def pos_embed():
        for c in range(NCH):
            b = PE_[:, c, 0:32] if c < 4 else PE_[:, c, 0:64]
            if c < 4:
                bc = AP(b.tensor, b.offset, [list(b.ap[0]), [1, 32], [0, 64]])
            else:
                bc = AP(b.tensor, b.offset, [list(b.ap[0]), [0, 32], [1, 64]])
            xv = X[:, c, :].rearrange("q (r w) -> q r w", w=64)
            p.op("dve", ["ST"], [f"X{c}"], lambda E: E.tensor_tensor(out=xv, in0=xv, in1=bc, op=ALU.add))
        p.op("dve", ["ST", "IDN"], ["START", "ST"], lambda E: E.memset(DUMMY[:], 0.0))

    p.op("act", ["PAR"], ["SBF"], lambda E: E.activation(out=SBF[:].rearrange("q k t -> q (k t)"), in_=par("cvec"),
                                                         func=AF.Silu))
    CL, HCL, HBA, HBX, NHCL, GTMP = (GC[:, i, :] for i in range(6))
    p.op("act", ["PAR"], ["GC4"], lambda E: E.activation(out=GTMP, in_=par("lam"), func=AF.Exp, scale=-1.0))
    p.op("act", ["GC4"], ["GC4b"], lambda E: E.activation(out=GTMP, in_=GTMP, func=AF.Ln, bias=1.0))
    p.op("dve", ["GC4b"], ["GC0"], lambda E: E.tensor_scalar(out=CL, in0=GTMP, scalar1=-RG_C, scalar2=None, op0=ALU.mult))
    p.op("dve", ["GC0"], ["GC1"], lambda E: E.tensor_scalar(out=HCL, in0=CL, scalar1=0.5, scalar2=None, op0=ALU.mult))
    p.op("dve", ["PAR"], ["GC2"], lambda E: E.tensor_scalar(out=HBA, in0=par("ba"), scalar1=0.5, scalar2=None, op0=ALU.mult))
    p.op("dve", ["PAR"], ["GC3"], lambda E: E.tensor_scalar(out=HBX, in0=par("bx"), scalar1=0.5, scalar2=None, op0=ALU.mult))
    p.op("dve", ["GC0"], ["GC5"], lambda E: E.tensor_scalar(out=NHCL, in0=CL, scalar1=-0.5, scalar2=None, op0=ALU.mult))
    GCK = ["GC0", "GC1", "GC2", "GC3", "GC5"]

    def ada(L, j0, j1, done_key=None):
        b0, pk = alloc_ps(1)
        n = j1 - j0
        for j in range(j0, j1):
            wt, wk = load_w(wada_d[L, j], 1024, extra=(done_key if j == j1 - 1 else None))

            def emit(E, j=j, wt=wt):
                last = None
                for kc in range(8):
                    last = E.matmul(PS[:, b0 * 512 + 2 * (j - j0):b0 * 512 + 2 * (j - j0) + 2],
                                    lhsT=wt[:, kc * 128:(kc + 1) * 128], rhs=SBF[:, kc, :], start=(kc == 0), stop=(kc == 7))
                return last
            p.op("pe", wk + ["SBF"], pk, emit)
        psv = PS[:, b0 * 512:b0 * 512 + 2 * n].rearrange("q (j t) -> q t j", t=2)
        o, _ = _PAR["bada"]
        for t in range(2):
            p.op("dve", pk + ["PAR"], [f"MOD{L}_{j0}_{t}"], lambda E, t=t: E.tensor_tensor(
                out=MOD[:, L, t, j0:j1], in0=psv[:, t, :], in1=PAR[:, o + L * 48 + j0:o + L * 48 + j1], op=ALU.add))
        ks = [f"MOD{L}_{j0}_{t}" for t in range(2)]
        for j in range(j0, j1):
            modkey[(L, j)] = ks
        return ks

    modkey = {}

    def mks(L, j0, j1):
        out = []
        for j in range(j0, j1):
            for k in modkey[(L, j)]:
                if k not in out:
                    out.append(k)
        return out

    def run(gen):
        for _ in gen:
            pass

    def interleave(*gens):
        gens = list(gens)
        while gens:
            for g in list(gens):
                try:
                    next(g)
                except StopIteration:
                    gens.remove(g)

    def mod(L, t, j):
        return MOD[:, L, t, j:j + 1]

    def derive(L, which, keys):
        o, _ = _PAR["ng"]
        if which == 1:
            p.op("dve", keys + ["PAR"], [f"DER{L}_0"], lambda E: E.scalar_tensor_tensor(
                out=DER[:, L, 0, :], in0=MOD[:, L, 0, 8:16], scalar=1.0, in1=PAR[:, o + L * 16:o + L * 16 + 8],
                op0=ALU.add, op1=ALU.mult))
            p.op("dve", keys + ["PAR"], [f"DER{L}_2"], lambda E: E.scalar_tensor_tensor(
                out=DER[:, L, 2, :], in0=MOD[:, L, 1, 8:16], scalar=1.0, in1=PAR[:, o + L * 16:o + L * 16 + 8],
                op0=ALU.add, op1=ALU.mult))
        else:
            p.op("dve", keys + ["PAR"], [f"DER{L}_1"], lambda E: E.scalar_tensor_tensor(
                out=DER[:, L, 1, :], in0=MOD[:, L, 0, 32:40], scalar=1.0, in1=PAR[:, o + L * 16 + 8:o + L * 16 + 16],
                op0=ALU.add, op1=ALU.mult))

    def norm(T, src, src_keys, A, SH, dst, dst_keys, sc_keys, final=False):
        b0, pk = alloc_ps(nbanks(T))
        held = set(range(b0, b0 + nbanks(T)))
        reserved.update(held)
        psv = PS[:, b0 * 512:b0 * 512 + T]
        for c in range(NCH):
            sq = W16[:, c % 2, 0:T]
            sk = f"W16_{c % 2}"
            if c % 2 == 0 or T < 1024:
                p.op("act", [src_keys[c]], [sk], lambda E, c=c, sq=sq: E.activation(out=sq, in_=src(c), func=AF.Square))
            else:
                p.op("dve", [src_keys[c]], [sk], lambda E, c=c, sq=sq: E.tensor_tensor(out=sq, in0=src(c), in1=src(c),
                                                                                     op=ALU.mult))

            def emit(E, c=c, sq=sq):
                last = None
                for (a, b) in tgs(T):
                    last = E.matmul(psv[:, a:b], lhsT=ONES[:], rhs=sq[:, a:b], start=(c == 0), stop=(c == NCH - 1))
                return last
            p.op("pe", [sk, "ONES"], pk, emit)
            yield
        p.op("act", pk + ["EPSC"], pk, lambda E: E.activation(out=psv, in_=psv, func=AF.Ln, scale=1.0 / D, bias=EPSC[:, 0:1]))
        p.op("act", pk, pk, lambda E: E.activation(out=psv, in_=psv, func=AF.Exp, scale=-0.5))
        HT = min(T, 1024)
        i = 0
        dk = dst_keys if callable(dst_keys) else (lambda c, a: [dst_keys[c]])
        last = (NCH - 1, T - HT)
        halves = list(range(0, T, HT))
        order = [(c, a) for c in range(NCH) for a in halves] if final else [(c, a) for a in halves for c in range(NCH)]
        last = order[-1]
        for (c, a) in order:
            if True:
                if final:
                    p.op("dve", [src_keys[c]] + pk + sc_keys, dk(c, a), lambda E, c=c, a=a: E.scalar_tensor_tensor(
                        out=dst(c)[:, a:a + HT], in0=src(c)[:, a:a + HT], scalar=A(c), in1=psv[:, a:a + HT],
                        op0=ALU.mult, op1=ALU.mult))
                    if (c, a) == last:
                        reserved.difference_update(held)
                    yield
                    continue
                tmp = W32[:, i % 2, 0:HT]
                tk = f"W32_{i % 2}"
                i += 1
                p.op("dve", [src_keys[c]] + pk + sc_keys, [tk], lambda E, c=c, a=a, tmp=tmp: E.scalar_tensor_tensor(
                    out=tmp, in0=src(c)[:, a:a + HT], scalar=A(c), in1=psv[:, a:a + HT], op0=ALU.mult, op1=ALU.mult))
                p.op("act", [tk] + sc_keys, dk(c, a), lambda E, c=c, a=a, tmp=tmp: E.activation(
                    out=dst(c)[:, a:a + HT], in_=tmp, func=AF.Identity, bias=SH(c), scale=1.0))
                if (c, a) == last:
                    reserved.difference_update(held)
                yield

    def rec_branch(T, hsrc, hkeys, U, Ukeys):
        o_w, _ = _PAR["rcw"]
        o_b, _ = _PAR["rcb"]
        HT = min(T, 1024)
        i = 0
        for c in range(NR):
            wt, wk = load_w(win_d[10 + c], 1024)
            psv, pk = mm_job(T, wt, wk, list(range(8)), hsrc, hkeys)
            w = [PAR[:, o_w + c * 4 + k:o_w + c * 4 + k + 1] for k in range(4)]
            bb = PAR[:, o_b + c:o_b + c + 1]
            for t0 in range(0, T, HT):
                t1 = t0 + HT
                acc = W32[:, i % 2, 0:HT]
                ak = f"W32_{i % 2}"
                i += 1
                jlo = 1 if t0 == 0 else 0

                def emitA(E, acc=acc, t0=t0, t1=t1, jlo=jlo, w=w, bb=bb, psv=psv):
                    if jlo:
                        E.activation(out=acc[:, 0:1], in_=psv[:, 0:1], func=AF.Identity, bias=bb, scale=0.0)
                    return E.activation(out=acc[:, jlo:HT], in_=psv[:, t0 + jlo - 1:t1 - 1], func=AF.Identity, bias=bb,
                                        scale=w[0])
                p.op("act", pk + ["PAR"], [ak], emitA)
                for k, off in ((2, 1), (3, 2)):
                    n = min(t1 + off, T) - (t0 + off)
                    p.op("dve", pk + [ak, "PAR"], [ak], lambda E, acc=acc, t0=t0, off=off, n=n, k=k, w=w, psv=psv:
                         E.scalar_tensor_tensor(out=acc[:, 0:n], in0=psv[:, t0 + off:t0 + off + n], scalar=w[k],
                                                in1=acc[:, 0:n], op0=ALU.mult, op1=ALU.add))
                p.op("dve", pk + [ak, "PAR"], [Ukeys[c]], lambda E, acc=acc, t0=t0, t1=t1, c=c, w=w, psv=psv:
                     E.scalar_tensor_tensor(out=U(c)[:, t0:t1], in0=psv[:, t0:t1], scalar=w[1], in1=acc, op0=ALU.mult,
                                            op1=ALU.add))

    def gate_phase(T, U, Ukeys, is_ctx, G=None, Gkeys=None):
        NH = 2 if T > 1024 else 1
        HT = T // NH
        unit = [0]
        Y0 = HF[:, 6 * 1024:6 * 1024 + T]
        Y0k = ["H6", "H7"]
        for c in range(NR):
            ks = kset(c)
            y1_tiles = {}
            for d in (0, 1):
                pss = []
                for g in (0, 1):
                    wt, wk = load_w(wgate_d[(d * 2 + g) * 10 + c], 384)
                    psv, pk = mm_job(T, wt, wk, ks, U, [Ukeys[k] for k in ks])
                    pss.append((psv, pk))
                (psR, pkR), (psI, pkI) = pss
                cl = CL[:, d * NR + c:d * NR + c + 1]
                hcl = HCL[:, d * NR + c:d * NR + c + 1]
                nhcl = NHCL[:, d * NR + c:d * NR + c + 1]
                hba = HBA[:, d * NR + c:d * NR + c + 1]
                hbx = HBX[:, d * NR + c:d * NR + c + 1]
                order = list(range(NH)) if d == 0 else list(range(NH - 1, -1, -1))
                tiles = []
                for hh in order:
                    i = unit[0] % 3
                    unit[0] += 1
                    if i < 2:
                        At = HF[:, (0 + i) * 1024:(0 + i) * 1024 + HT]
                        St = HF[:, (2 + i) * 1024:(2 + i) * 1024 + HT]
                        It = HF[:, (4 + i) * 1024:(4 + i) * 1024 + HT]
                        Ak, Sk, Ik = f"H{i}", f"H{2 + i}", f"H{4 + i}"
                    else:
                        At, St, It = W16F[0][:, 0:HT], W16F[1][:, 0:HT], START[:, 0:HT]
                        Ak, Sk, Ik = "W16_0", "W16_1", "START"
                    a0 = hh * HT
                    p.op("act", pkR + GCK, [Ak], lambda E, At=At, a0=a0, hba=hba, psR=psR: E.activation(
                        out=At, in_=psR[:, a0:a0 + HT], func=AF.Tanh, bias=hba, scale=0.5))
                    p.op("act", pkI + GCK, [Ik], lambda E, It=It, a0=a0, hbx=hbx, psI=psI: E.activation(
                        out=It, in_=psI[:, a0:a0 + HT], func=AF.Tanh, bias=hbx, scale=0.5))
                    last_half = (hh == order[-1])
                    if last_half and not is_ctx:
                        yield
                    p.op("act", [Ak] + GCK, [Sk], lambda E, At=At, St=St, cl=cl: E.activation(
                        out=St, in_=At, func=AF.Exp, bias=cl, scale=cl))
                    p.op("act", [Ak] + GCK, [Ak], lambda E, At=At, hcl=hcl: E.activation(
                        out=At, in_=At, func=AF.Exp, bias=hcl, scale=hcl))
                    tiles.append((hh, At, St, It, Ak, Sk, Ik))
                for (hh, At, St, It, Ak, Sk, Ik) in tiles:
                    p.op("act", [Sk], [Sk], lambda E, St=St: E.activation(out=St, in_=St, func=AF.Sqrt, bias=1.0,
                                                                         scale=-1.0))
                prev = None
                for (hh, At, St, It, Ak, Sk, Ik) in tiles:
                    a0 = hh * HT
                    p.op("dve", [Ik, Sk], [Ik], lambda E, It=It, St=St: E.scalar_tensor_tensor(
                        out=It, in0=It, scalar=1.0, in1=St, op0=ALU.add, op1=ALU.mult))
                    p.op("dve", [Ik, Ukeys[c]], [Ik], lambda E, It=It, a0=a0, c=c: E.scalar_tensor_tensor(
                        out=It, in0=It, scalar=0.5, in1=U(c)[:, a0:a0 + HT], op0=ALU.mult, op1=ALU.mult))
                    if d == 0:
                        yt = Y0[:, a0:a0 + HT]
                        ykeys = Y0k
                    else:
                        j = hh % 2
                        yt = W32[:, j, 0:HT]
                        ykeys = [f"W32_{j}"]
                        y1_tiles[hh] = (yt, ykeys)
                    if prev is None:
                        init = 0.0 if is_ctx else H0[:, d, c:c + 1]
                        ikeys = [] if is_ctx else ["H0"]
                    else:
                        pyt, pykeys = prev
                        init = pyt[:, HT - 1:HT] if d == 0 else pyt[:, 0:1]
                        ikeys = pykeys
                    if d == 0:
                        p.op("dve", [Ak, Ik] + ikeys, ykeys, lambda E, yt=yt, At=At, It=It, init=init:
                             E.tensor_tensor_scan(out=yt, data0=At, data1=It, initial=init, op0=ALU.mult, op1=ALU.add))
                    else:
                        p.op("dve", [Ak, Ik] + ikeys, ykeys, lambda E, yt=yt, At=At, It=It, init=init:
                             E.tensor_tensor_scan(out=yt[:, ::-1], data0=At[:, ::-1], data1=It[:, ::-1], initial=init,
                                                  op0=ALU.mult, op1=ALU.add))
                    prev = (yt, ykeys)
                if is_ctx:
                    yt, ykeys = prev
                    col = yt[:, HT - 1:HT] if d == 0 else yt[:, 0:1]
                    p.op("dve", ykeys, ["H0"], lambda E, col=col, d=d, c=c: E.tensor_copy(out=H0[:, d, c:c + 1], in_=col))
            if not is_ctx:
                for hh in range(NH - 1, -1, -1):
                    yt, ykeys = y1_tiles[hh]
                    a0 = hh * HT
                    p.op("dve", ykeys + Y0k, ykeys, lambda E, yt=yt, a0=a0: E.tensor_tensor(
                        out=yt, in0=yt, in1=Y0[:, a0:a0 + HT], op=ALU.add))
                    p.op("dve", ykeys + [Gkeys[c]], [Gkeys[c]], lambda E, yt=yt, a0=a0, c=c: E.tensor_tensor(
                        out=G(c)[:, a0:a0 + HT], in0=yt, in1=G(c)[:, a0:a0 + HT], op=ALU.mult))

    def gate_phase_ctx(U, Ukeys):
        T = T_CTX
        units = [(c, d) for d in (0, 1) for c in range(NR)]
        cA = lambda u: GUF[:, 5120 + u * 256:5120 + (u + 1) * 256]
        cS = lambda u: GUF[:, 10240 + u * 256:10240 + (u + 1) * 256]
        cI = lambda u: GUF[:, 15360 + u * 256:15360 + (u + 1) * 256]
        cTh = [GUF[:, 4352:4608], GUF[:, 4608:4864]]
        cY = GUF[:, 4864:5120]
        allk = []
        WV = 2
        for w0 in range(0, len(units), WV):
            wave = []
            for u in range(w0, w0 + WV):
                c, d = units[u]
                ks = kset(c)
                pss = []
                for g in (0, 1):
                    wt, wk = load_w(wgate_d[(d * 2 + g) * 10 + c], 384)
                    psv, pk = mm_job(T, wt, wk, ks, U, [Ukeys[k] for k in ks])
                    pss.append((psv, pk))
                sc = [t_[:, d * NR + c:d * NR + c + 1] for t_ in (CL, HCL, HBA, HBX)]
                wave.append((u, pss, sc))
                allk.extend([f"cA{u}", f"cS{u}", f"cI{u}"])
            for (u, pss, sc) in wave:
                p.op("act", pss[0][1] + GCK, [f"cA{u}"], lambda E: E.activation(out=cA(u), in_=pss[0][0], func=AF.Tanh,
                                                                                  bias=sc[2], scale=0.5))
            for (u, pss, sc) in wave:
                p.op("act", pss[1][1] + GCK, [f"cI{u}"], lambda E: E.activation(out=cI(u), in_=pss[1][0], func=AF.Tanh,
                                                                                  bias=sc[3], scale=0.5))
            for (u, pss, sc) in wave:
                p.op("act", [f"cA{u}"] + GCK, [f"cS{u}"], lambda E: E.activation(out=cS(u), in_=cA(u), func=AF.Exp,
                                                                                   bias=sc[0], scale=sc[0]))
            for (u, pss, sc) in wave:
                p.op("act", [f"cA{u}"] + GCK, [f"cA{u}"], lambda E: E.activation(out=cA(u), in_=cA(u), func=AF.Exp,
                                                                                   bias=sc[1], scale=sc[1]))
            yield
        NU = len(units)
        cAall = GUF[:, 5120:5120 + NU * 256]
        cSall = GUF[:, 10240:10240 + NU * 256]
        cIall = GUF[:, 15360:15360 + NU * 256]
        Aks = [f"cA{u}" for u in range(NU)]
        Sks = [f"cS{u}" for u in range(NU)]
        Iks = [f"cI{u}" for u in range(NU)]
        p.op("act", Sks, Sks, lambda E: E.activation(out=cSall, in_=cSall, func=AF.Sqrt, bias=1.0, scale=-1.0))
        yield
        hw = NU // 2 * 256
        p.op("dve", Aks, Aks, lambda E: E.memset(cAall[:, 0:hw:256], 0.0))
        p.op("dve", Aks, Aks, lambda E: E.memset(cAall[:, hw + 255:2 * hw:256], 0.0))
        p.op("dve", Iks + Sks, Iks, lambda E: E.scalar_tensor_tensor(out=cIall, in0=cIall, scalar=1.0, in1=cSall,
                                                                       op0=ALU.add, op1=ALU.mult))
        ucall = GU2[:, 3 * 2048:3 * 2048 + NR * T_CTX]
        for d in range(2):
            p.op("dve", Iks + ["GU3", "GU4"], Iks, lambda E: E.scalar_tensor_tensor(
                out=cIall[:, d * hw:(d + 1) * hw], in0=cIall[:, d * hw:(d + 1) * hw], scalar=0.5, in1=ucall,
                op0=ALU.mult, op1=ALU.mult))
        yield
        p.op("dve", Aks + Iks + Sks, Sks, lambda E: E.tensor_tensor_scan(
            out=cSall[:, 0:hw], data0=cAall[:, 0:hw], data1=cIall[:, 0:hw], initial=0.0, op0=ALU.mult, op1=ALU.add))
        p.op("dve", Aks + Iks + Sks, Sks, lambda E: E.tensor_tensor_scan(
            out=cSall[:, hw:2 * hw][:, ::-1], data0=cAall[:, hw:2 * hw][:, ::-1], data1=cIall[:, hw:2 * hw][:, ::-1],
            initial=0.0, op0=ALU.mult, op1=ALU.add))
        p.op("dve", Sks, ["H0"], lambda E: E.tensor_copy(out=H0[:, 0, :], in_=cSall[:, 255:hw:256]))
        p.op("dve", Sks + ["H0"], ["H0"], lambda E: E.tensor_copy(out=H0[:, 1, :], in_=cSall[:, hw:2 * hw:256]))
        yield
        p.op("dve", allk + ["cY", "cTh0", "cTh1"], [f"GU{j}" for j in range(4, 20)], lambda E: E.memset(DUMMY[:], 0.0))

    def mlp(L, gkeys):
        i = 0
        for q in range(2):
            for m in range(16):
                wt, wk = load_w(mlpin_d[L, q * 16 + m], 1024)
                for a0 in (0, 1024):
                    psv, pk = mm_job(1024, wt, wk, list(range(8)), lambda k, a0=a0: H[:, k, a0:a0 + 1024], Hhalf(a0))
                    rt = W16[:, i % 2, 0:1024]
                    rk = f"W16_{i % 2}"
                    i += 1
                    p.op("act", pk, [rk], lambda E, rt=rt, psv=psv: E.activation(out=rt, in_=psv, func=AF.Relu))
                    p.op("dve", [rk], [f"GU{m}"], lambda E, rt=rt, m=m, a0=a0: E.tensor_tensor(
                        out=GU[:, m, a0:a0 + 1024], in0=rt, in1=rt, op=ALU.mult))
            for mo in range(NCH):
                wt, wk = load_w(mlpout_d[L, q * 8 + mo], 2048)
                psv, pk = mm_job(T_LAT, wt, wk, list(range(16)), lambda k: GU[:, k, :], [f"GU{k}" for k in range(16)])
                p.op("dve", pk + gkeys + [f"X{mo}"], [f"X{mo}"], lambda E, mo=mo, psv=psv: E.scalar_tensor_tensor(
                    out=X[:, mo, :], in0=psv, scalar=mod(L, 0, 40 + mo), in1=X[:, mo, :], op0=ALU.mult, op1=ALU.add))

    Xk = [f"X{c}" for c in range(NCH)]
    Hk = [f"H{c}" for c in range(NCH)] + [f"HB{c}" for c in range(NCH)]
    hkey = lambda c, a: [f"H{c}"] if a == 0 else [f"HB{c}"]
    Hhalf = lambda a: [hkey(k, a)[0] for k in range(NCH)]

    def h_fence():
        p.op("dve", [], Hk, lambda E: E.memset(DUMMY[:], 0.0))

    L = 0
    mk_a = ["MOD0_0_0", "MOD0_0_1"]
    g_cn = norm(T_CTX, lambda c: XC[:, c, :], ["GU0"] * 4 + ["GU1"] * 4, lambda c: DER[:, 0, 2, c:c + 1],
                lambda c: mod(0, 1, c), lambda c: HC[:, c, :], ["GU2"] * 8, mk_a + ["DER0_2"])
    for _ in range(NCH):
        next(g_cn)
    ada(0, 0, 16, done_key="ADA0_DONE")
    assert mks(0, 0, 16) == mk_a
    for c in range(NCH):
        p.dma("sp", f"xin{c}", [(X[:, c, :], xT_v[:, c, :])], ["ADA0_DONE"], [f"X{c}"])
    derive(0, 1, mk_a)
    run(g_cn)
    UCk = ["GU3"] * 8 + ["GU4"] * 2
    rec_branch(T_CTX, lambda k: HC[:, k, :], ["GU2"], lambda c: UC[:, c, :], UCk)
    pos_embed()
    g_cg = gate_phase_ctx(lambda c: UC[:, c, :], UCk)
    g_ln = norm(T_LAT, lambda c: X[:, c, :], Xk, lambda c: DER[:, 0, 0, c:c + 1], lambda c: mod(0, 0, c),
                lambda c: H[:, c, :], hkey, mk_a + ["DER0_0"])
    for _ in range(3):
        next(g_cg)
    for _ in g_cg:
        for _ in range(3):
            next(g_ln, None)
    run(g_ln)
    Uk = [f"GU{10 + c}" for c in range(NR)]
    Gk = [f"GU{c}" for c in range(NR)]
    rec_branch(T_LAT, lambda k: H[:, k, :], Hk, lambda c: GU[:, 10 + c, :], Uk)
    for c in range(NR):
        wt, wk = load_w(win_d[c], 1024)
        psv, pk = mm_job(T_LAT, wt, wk, list(range(8)), lambda k: H[:, k, :], Hk)
        p.op("act", pk, [Gk[c]], lambda E, c=c, psv=psv: E.activation(out=GU[:, c, :], in_=psv, func=AF.Gelu_apprx_tanh))
    groups = [(0, j, j + 4) for j in range(16, 48, 4)]
    if n_layers > 1:
        groups += [(1, j, j + 4) for j in range(0, 48, 4)]
    gi_ = iter(groups)
    h_fence()
    for _ in gate_phase(T_LAT, lambda c: GU[:, 10 + c, :], Uk, False, lambda c: GU[:, c, :], Gk):
        g_ = next(gi_, None)
        if g_ is not None:
            ada(*g_)
    for g_ in gi_:
        ada(*g_)
    h_fence()
    mk_b = mks(0, 16, 48)
    derive(0, 2, mk_b)
    if n_layers > 1:
        mk1_a = mks(1, 0, 24)
        mk1_b = mks(1, 24, 48)
    for mo in range(NCH):
        wt, wk = load_w(wout_d[mo], 1280)
        psv, pk = mm_job(T_LAT, wt, wk, list(range(NR)), lambda k: GU[:, k, :], Gk)
        p.op("dve", pk + mk_b + [f"X{mo}"], [f"X{mo}"], lambda E, mo=mo, psv=psv: E.scalar_tensor_tensor(
            out=X[:, mo, :], in0=psv, scalar=mod(0, 0, 16 + mo), in1=X[:, mo, :], op0=ALU.mult, op1=ALU.add))
    run(norm(T_LAT, lambda c: X[:, c, :], Xk, lambda c: DER[:, 0, 1, c:c + 1], lambda c: mod(0, 0, 24 + c),
             lambda c: H[:, c, :], hkey, mk_b + ["DER0_1"]))
    mlp(0, mk_b)

    if n_layers > 1:
        L = 1
        derive(1, 1, mk1_a)
        derive(1, 2, mk1_b)
        o_b2, _ = _PAR["bpw2"]
        p.op("dve", mk1_a + ["PAR"], ["DER1_3"], lambda E: E.tensor_tensor(
            out=DER[:, 1, 3, :], in0=MOD[:, 1, 0, 16:24], in1=PAR[:, o_b2:o_b2 + 8], op=ALU.mult))
        run(norm(T_LAT, lambda c: X[:, c, :], Xk, lambda c: DER[:, 1, 0, c:c + 1], lambda c: mod(1, 0, c),
                 lambda c: H[:, c, :], hkey, mk1_a + ["DER1_0"]))
        for i in range(2):
            def emit(E, i=i):
                E.memset(W16[:, i, 0:15], 0.0)
                return E.memset(W16[:, i, 15 + T_LAT:2080], 0.0)
            p.op("dve", [], [f"W16_{i}", f"Z{i}a", f"Z{i}b"], emit)
        DGB = [GU2[:, (16 + 2 * b) * 2048:(16 + 2 * b) * 2048 + KW * 128].rearrange("q (k m) -> q k m", k=KW) for b in range(2)]
        o_cw, _ = _PAR["ccw"]
        o_cb, _ = _PAR["ccb"]
        o_b1, _ = _PAR["bpw1"]
        ZC = GUF[:, 0:16 * 1024].rearrange("q (c t) -> q c t", c=NCH)
        gi = 0
        for c in range(NCH):
            db = c % 2
            dkeys = [f"GU{16 + 2 * db}", f"GU{17 + 2 * db}"]

            def emit(E, c=c, db=db):
                last = None
                for k in range(NPE):
                    last = E.tensor_scalar(out=DGB[db][:, k, :], in0=IDN[:], scalar1=PAR[:, o_cw + c * KW + k:o_cw + c * KW + k + 1],
                                           scalar2=None, op0=ALU.mult)
                return last
            p.op("dve", ["IDN", "PAR"], dkeys, emit)
            wtA, wkA = load_w(pw1_d[2 * c], 1024)
            wtB, wkB = load_w(pw1_d[2 * c + 1], 1024)
            zt = W16[:, c % 2, :]
            zka, zkb = f"Z{c % 2}a", f"Z{c % 2}b"
            for a0 in (0, 1024):
                psA, pkA = mm_job(1024, wtA, wkA, list(range(8)), lambda k, a0=a0: H[:, k, a0:a0 + 1024], Hhalf(a0))
                psB, pkB = mm_job(1024, wtB, wkB, list(range(8)), lambda k, a0=a0: H[:, k, a0:a0 + 1024], Hhalf(a0))
                sg = W32[:, gi % 2, :]
                sgk = f"W32_{gi % 2}"
                gi += 1
                p.op("act", pkB + ["PAR"], [sgk], lambda E, sg=sg, c=c, psB=psB: E.activation(
                    out=sg, in_=psB, func=AF.Sigmoid, bias=PAR[:, o_b1 + 8 + c:o_b1 + 9 + c], scale=1.0))
                p.op("dve", pkA + [sgk, "PAR"], [zka if a0 == 0 else zkb],
                     lambda E, sg=sg, a0=a0, c=c, psA=psA, zt=zt: E.scalar_tensor_tensor(
                         out=zt[:, 15 + a0:15 + a0 + 1024], in0=psA, scalar=PAR[:, o_b1 + c:o_b1 + c + 1],
                         in1=sg, op0=ALU.add, op1=ALU.mult))
            zkeys_tg = [[zka], [zka, zkb], [zka, zkb], [zkb]]
            for tg in range(4):
                b0, pk = alloc_ps(1)
                pv = PS[:, b0 * 512:(b0 + 1) * 512]
                bd, pkd = alloc_ps(1)
                pd = PS[:, bd * 512:(bd + 1) * 512]
                zck = f"GU{2 * c + tg // 2}"
                zcv = ZC[:, c, tg * 512:(tg + 1) * 512]

                def emit(E, tg=tg, db=db, pv=pv, zt=zt):
                    last = None
                    for k in range(NPE):
                        last = E.matmul(pv, lhsT=DGB[db][:, k, :], rhs=zt[:, tg * 512 + k:tg * 512 + k + 512],
                                        start=(k == 0), stop=(k == NPE - 1))
                    return last
                p.op("pe", dkeys + zkeys_tg[tg], pk, emit)

                tapsA = list(range(NPE, KW, 2))
                tapsB = list(range(NPE + 1, KW, 2))
                for i in range(max(len(tapsA), len(tapsB))):
                    if i < len(tapsA):
                        k = tapsA[i]
                        wk_ = PAR[:, o_cw + c * KW + k:o_cw + c * KW + k + 1]
                        src = zt[:, tg * 512 + k:tg * 512 + k + 512]
                        if i == 0:
                            p.op("dve", zkeys_tg[tg] + ["PAR"], pkd, lambda E: E.tensor_scalar(
                                out=pd, in0=src, scalar1=wk_, scalar2=PAR[:, o_cb + c:o_cb + c + 1], op0=ALU.mult, op1=ALU.add))
                        else:
                            p.op("dve", zkeys_tg[tg] + ["PAR"] + pkd, pkd, lambda E: E.scalar_tensor_tensor(
                                out=pd, in0=src, scalar=wk_, in1=pd, op0=ALU.mult, op1=ALU.add))
                    if i < len(tapsB):
                        k = tapsB[i]
                        wk_ = PAR[:, o_cw + c * KW + k:o_cw + c * KW + k + 1]
                        src = zt[:, tg * 512 + k:tg * 512 + k + 512]
                        if i == 0:
                            p.op("dve", zkeys_tg[tg] + ["PAR"], [zck], lambda E: E.tensor_scalar(
                                out=zcv, in0=src, scalar1=wk_, scalar2=None, op0=ALU.mult))
                        else:
                            p.op("dve", zkeys_tg[tg] + ["PAR", zck], [zck], lambda E: E.scalar_tensor_tensor(
                                out=zcv, in0=src, scalar=wk_, in1=zcv, op0=ALU.mult, op1=ALU.add))
                p.op("dve", pkd + [zck], [zck], lambda E: E.tensor_tensor(out=zcv, in0=zcv, in1=pd, op=ALU.add))
                p.op("dve", pk + [zck], [zck], lambda E, pv=pv, zcv=zcv: E.tensor_tensor(out=zcv, in0=zcv, in1=pv, op=ALU.add))
        p.op("dve", [], ["W16_0", "W16_1", "Z0a", "Z0b", "Z1a", "Z1b"], lambda E: E.memset(DUMMY[:], 0.0))
        o_g, _ = _PAR["lng"]
        o_lb, _ = _PAR["lnb"]
        st = []
        for hf in range(2):
            a0 = hf * 1024
            b1, pk1 = alloc_ps(2)
            b2, pk2 = alloc_ps(2)
            S1 = PS[:, b1 * 512:b1 * 512 + 1024]
            S2 = PS[:, b2 * 512:b2 * 512 + 1024]
            st.append((a0, pk1, pk2, S1, S2))
            for c in range(NCH):
                zk = f"GU{2 * c + hf}"
                tb = W16[:, c % 2, 0:1024]
                tq = W16[:, c % 2, 1024:2048]
                tk = f"W16_{c % 2}"
                p.op("act", [zk], [tk], lambda E, c=c, tb=tb: E.activation(out=tb, in_=ZC[:, c, a0:a0 + 1024],
                                                                               func=AF.Identity))
                p.op("act", [zk, tk], [tk], lambda E, c=c, tq=tq: E.activation(out=tq, in_=ZC[:, c, a0:a0 + 1024],
                                                                               func=AF.Square))

                def emit(E, c=c, tb=tb, tq=tq):
                    last = None
                    for t in range(2):
                        E.matmul(S1[:, t * 512:(t + 1) * 512], lhsT=ONES[:], rhs=tb[:, t * 512:(t + 1) * 512],
                                 start=(c == 0), stop=(c == NCH - 1))
                        last = E.matmul(S2[:, t * 512:(t + 1) * 512], lhsT=ONES[:], rhs=tq[:, t * 512:(t + 1) * 512],
                                        start=(c == 0), stop=(c == NCH - 1))
                    return last
                p.op("pe", [tk, "ONES"], pk1 + pk2, emit)
        for hf in range(2):
            a0, pk1, pk2, S1, S2 = st[hf]
            mean = W32[:, 0, :]
            var = W32[:, 1, :]
            p.op("act", pk1, ["W32_0"], lambda E: E.activation(out=mean, in_=S1, func=AF.Identity, scale=1.0 / D))
            p.op("dve", ["W32_0"], ["W32_1"], lambda E: E.tensor_tensor(out=var, in0=mean, in1=mean, op=ALU.mult))
            p.op("dve", pk2 + ["W32_1"], ["W32_1"], lambda E: E.scalar_tensor_tensor(
                out=var, in0=S2, scalar=1.0 / D, in1=var, op0=ALU.mult, op1=ALU.subtract))
            p.op("act", ["W32_1", "EPSC"], ["W32_1"], lambda E: E.activation(out=var, in_=var, func=AF.Ln, bias=EPSC[:, 0:1], scale=1.0))
            p.op("act", ["W32_1"], pk2, lambda E: E.activation(out=S2, in_=var, func=AF.Exp, scale=-0.5))
            p.op("dve", ["W32_0"] + pk2, pk1, lambda E: E.scalar_tensor_tensor(
                out=S1, in0=mean, scalar=-1.0, in1=S2, op0=ALU.mult, op1=ALU.mult))
        for hf in range(2):
            a0, pk1, pk2, S1, S2 = st[hf]
            for c in range(NCH):
                zk = f"GU{2 * c + hf}"
                zz = ZC[:, c, a0:a0 + 1024]
                p.op("dve", [zk] + pk2, [zk], lambda E, zz=zz: E.tensor_tensor(out=zz, in0=zz, in1=S2, op=ALU.mult))
                p.op("dve", [zk] + pk1, [zk], lambda E, zz=zz: E.tensor_tensor(out=zz, in0=zz, in1=S1, op=ALU.add))
                p.op("act", [zk, "PAR"], hkey(c, a0), lambda E, zz=zz, c=c: E.activation(
                    out=H[:, c, a0:a0 + 1024], in_=zz, func=AF.Silu, bias=PAR[:, o_lb + c:o_lb + c + 1],
                    scale=PAR[:, o_g + c:o_g + c + 1]))
        for mo in range(NCH):
            p.op("act", ["DER1_3", f"X{mo}"], [f"X{mo}"], lambda E, mo=mo: E.activation(
                out=X[:, mo, :], in_=X[:, mo, :], func=AF.Identity, bias=DER[:, 1, 3, mo:mo + 1], scale=1.0))
        for a0 in (0, 1024):
            for mo in range(NCH):
                wt, wk = load_w(pw2_d[mo], 1024)
                psv, pk = mm_job(1024, wt, wk, list(range(8)), lambda k, a0=a0: H[:, k, a0:a0 + 1024], Hhalf(a0))
                p.op("dve", pk + mk1_a + [f"X{mo}"], [f"X{mo}"], lambda E, mo=mo, psv=psv, a0=a0: E.scalar_tensor_tensor(
                    out=X[:, mo, a0:a0 + 1024], in0=psv, scalar=mod(1, 0, 16 + mo), in1=X[:, mo, a0:a0 + 1024],
                    op0=ALU.mult, op1=ALU.add))
        run(norm(T_LAT, lambda c: X[:, c, :], Xk, lambda c: DER[:, 1, 1, c:c + 1], lambda c: mod(1, 0, 24 + c),
                 lambda c: H[:, c, :], hkey, mk1_b + ["DER1_1"]))
        mlp(1, mk1_b)

    o_fg, _ = _PAR["fg"]
    run(norm(T_LAT, lambda c: X[:, c, :], Xk, lambda c: PAR[:, o_fg + c:o_fg + c + 1], None,
             lambda c: X[:, c, :], Xk, ["PAR"], final=True))
    outT_v = outT.rearrange("(c q) t -> q c t", q=128)
    toks = [p.dma("sp" if c % 2 == 0 else "act", f"out{c}", [(outT_v[:, c, :], X[:, c, :])], [Xk[c]], []) for c in range(NCH)]
    for tok in toks:
        nc.sync.wait_ge(tok[0], tok[1])
    return nc, es


def _fm(v, nchunk):
    return np.ascontiguousarray(np.asarray(v, np.float32).reshape(nchunk, 128).T)


def _tile(W, rows, cols):
    sub = W[rows][:, cols]
    kc = sub.shape[0] // 128
    return np.ascontiguousarray(sub.reshape(kc, 128, sub.shape[1]).transpose(1, 0, 2).reshape(128, kc * sub.shape[1]))


_NC_CACHE = {}


def _get_nc(n_layers):
    if n_layers not in _NC_CACHE:
        _NC_CACHE[n_layers] = build(n_layers)
    return _NC_CACHE[n_layers][0]


def prep_inputs(x, c, ctx, c_ctx, w_ada, b_ada, norm_g, rec_w_in, rec_conv_w, rec_conv_b, rec_lambda,
                rec_w_a, rec_b_a, rec_w_x, rec_b_x, rec_w_out, conf_w_pw1, conf_b_pw1, conf_conv_w, conf_conv_b,
                conf_ln_g, conf_ln_b, conf_w_pw2, conf_b_pw2, mlp_w_in, mlp_w_out, final_g):
    f = lambda a: np.asarray(a, np.float32)
    x, c, ctx, c_ctx = f(x), f(c), f(ctx), f(c_ctx)
    w_ada, mlp_w_in, mlp_w_out = f(w_ada), f(mlp_w_in), f(mlp_w_out)
    sl = lambda i: slice(i * 128, (i + 1) * 128)
    allk = lambda K: slice(0, K)
    wada = np.stack([np.stack([_tile(w_ada[L], allk(D), sl(j)) for j in range(48)]) for L in range(2)])
    win = np.stack([_tile(f(rec_w_in)[0], allk(D), sl(m)) for m in range(20)])
    wgate = np.zeros((40, 128, 384), np.float32)
    for d in range(2):
        for g, w in enumerate((f(rec_w_a)[0, d], f(rec_w_x)[0, d])):
            Wbd = np.zeros((R, R), np.float32)
            for h in range(16):
                Wbd[h * 80:(h + 1) * 80, h * 80:(h + 1) * 80] = w[h]
            for cc in range(NR):
                ks = kset(cc)
                for idx, k in enumerate(ks):
                    wgate[(d * 2 + g) * 10 + cc, :, idx * 128:(idx + 1) * 128] = Wbd[sl(k), sl(cc)]
    wout = np.stack([_tile(f(rec_w_out)[0], allk(R), sl(m)) for m in range(8)])
    mlpin = np.stack([np.stack([_tile(mlp_w_in[L], allk(D), sl(m)) for m in range(32)]) for L in range(2)])
    mlpout = np.stack([np.stack([_tile(mlp_w_out[L], slice(q * 2048, (q + 1) * 2048), sl(mo))
                                 for q in range(2) for mo in range(8)]) for L in range(2)])
    pw1 = np.stack([_tile(f(conf_w_pw1)[0], allk(D), sl(m)) for c8 in range(8) for m in (c8, 8 + c8)])
    pw2 = np.stack([_tile(f(conf_w_pw2)[0], allk(D), sl(m)) for m in range(8)])

    shared = dict(wada=wada, win=win, wgate=wgate, wout=wout, mlpin=mlpin, mlpout=mlpout, pw1=pw1, pw2=pw2)
    pbase = np.zeros((128, NPAR), np.float32)

    def put(name, arr):
        o, n = _PAR[name]
        pbase[:, o:o + n] = np.asarray(arr, np.float32).reshape(128, n)
    put("bada", np.stack([_fm(f(b_ada)[L], 48) for L in range(2)], axis=1))
    put("ng", np.stack([_fm(f(norm_g)[L, n], 8) for L in range(2) for n in range(2)], axis=1))
    put("fg", _fm(final_g, 8))
    put("rcw", np.stack([_fm(f(rec_conv_w)[0, k], NR) for k in range(4)], axis=2))
    put("rcb", _fm(f(rec_conv_b)[0], NR))
    put("lam", np.stack([_fm(f(rec_lambda)[0, d], NR) for d in range(2)], axis=1))
    put("ba", np.stack([_fm(f(rec_b_a)[0, d].reshape(-1), NR) for d in range(2)], axis=1))
    put("bx", np.stack([_fm(f(rec_b_x)[0, d].reshape(-1), NR) for d in range(2)], axis=1))
    put("bpw1", _fm(f(conf_b_pw1)[0], 16))
    put("ccw", np.stack([_fm(f(conf_conv_w)[0, k], 8) for k in range(KW)], axis=2))
    put("ccb", _fm(f(conf_conv_b)[0], 8))
    put("lng", _fm(f(conf_ln_g)[0], 8))
    put("lnb", _fm(f(conf_ln_b)[0], 8))
    put("bpw2", _fm(f(conf_b_pw2)[0], 8))
    in_maps = []
    for b in range(8):
        pb = pbase.copy()
        o, n = _PAR["cvec"]
        pb[:, o:o + n] = np.stack([_fm(c[b], 8), _fm(c_ctx, 8)], axis=2).reshape(128, 16)
        m = dict(shared)
        m["xT"] = np.ascontiguousarray(x[b].T)
        m["ctxT"] = np.ascontiguousarray(ctx[b].T)
        m["par"] = pb
        in_maps.append(m)
    return in_maps


def run(inputs, n_layers=2, trace=False):
    nc = _get_nc(n_layers)
    in_maps = prep_inputs(**inputs)
    res = run_bass_kernel_spmd(nc, in_maps, core_ids=list(range(8)), trace=trace)
    out = np.stack([np.ascontiguousarray(r["outT"].T) for r in res.results]).astype(np.float32)
    return out, res


def kernel(**inputs):
    out, _ = run(inputs, 2)
    return out
```
